# Optimizing a Trainium2 kernel written in Bass

```python
import jax, jax.numpy as jnp
from jax import lax
import numpy as np

D_MODEL = 1024
BATCH = 8
SEQ = 8192
DEPTH = 2

GRID_W = 64
CTX_LEN = 256
N_EVEN = (DEPTH + 1) // 2
N_ODD = DEPTH // 2
EPS = 1e-6

POOL_WINDOWS = (2, 4, 8, 16)
POOL_GROUPS = len(POOL_WINDOWS)
POOL_GROUP_DIM = D_MODEL // 16
POOL_DIM = POOL_GROUPS * POOL_GROUP_DIM

HEAD_DIM = 128
N_Q_HEADS = 6
N_KV_HEADS = 2
Q_PER_KV = N_Q_HEADS // N_KV_HEADS
ATTN_DIM = N_Q_HEADS * HEAD_DIM
KV_DIM = N_KV_HEADS * HEAD_DIM
ROPE_THETA = 10000.0
Q_BLOCK = 128
MIX_WIDTH = POOL_DIM + ATTN_DIM
IN_A_WIDTH = POOL_DIM + ATTN_DIM + 2 * KV_DIM

D_INNER = 2 * D_MODEL
SSM_HEAD_DIM = 64
SSM_HEADS = D_INNER // SSM_HEAD_DIM
SSM_GROUPS = 4
HEADS_PER_GROUP = SSM_HEADS // SSM_GROUPS
D_STATE = 128
D_CONV = 4
SSD_CHUNK = 128
CONV_DIM = D_INNER + 2 * SSM_GROUPS * D_STATE
IN_C_WIDTH = D_INNER + CONV_DIM + 2 * SSM_HEADS

D_FF = 2816
FFN_CONV = 3

kernel_name = "hybrid_pool_attn_ssd_convffn_dit"


def rms_norm(x, w):
    xf = x.astype(jnp.float32)
    y = xf * lax.rsqrt(jnp.mean(xf * xf, axis=-1, keepdims=True) + EPS)
    return (y * w.astype(jnp.float32)).astype(x.dtype)


def adaln(cond, w, b):
    return jax.nn.silu(cond) @ w + b


def modulate(h, shift, scale):
    return h * (1.0 + scale) + shift


def depthwise_conv_centred(u, w, b):
    T = u.shape[1]
    K = w.shape[1]
    left = K // 2
    right = K - 1 - left
    up = jnp.pad(u, ((0, 0), (left, right), (0, 0)))
    out = b
    for k in range(K):
        out = out + up[:, k:k + T] * w[:, k]
    return out


def axial_rope_tables(T):
    rows = T // GRID_W
    row = jnp.repeat(jnp.arange(rows, dtype=jnp.float32), GRID_W)
    col = jnp.tile(jnp.arange(GRID_W, dtype=jnp.float32), rows)
    half = HEAD_DIM // 2
    inv_freq = ROPE_THETA ** (-jnp.arange(0, half, 2, dtype=jnp.float32) / half)
    ang = jnp.concatenate([row[:, None] * inv_freq, col[:, None] * inv_freq], axis=-1)
    return jnp.cos(ang), jnp.sin(ang)


def apply_rope(x, cos, sin):
    cos = cos[None, :, None, :].astype(x.dtype)
    sin = sin[None, :, None, :].astype(x.dtype)
    x1 = x[..., 0::2]
    x2 = x[..., 1::2]
    r1 = x1 * cos - x2 * sin
    r2 = x1 * sin + x2 * cos
    return jnp.stack([r1, r2], axis=-1).reshape(x.shape)


def centred_window_mean(u, w):
    T = u.shape[1]
    left = w // 2
    right = w - 1 - left
    up = jnp.pad(u.astype(jnp.float32), ((0, 0), (left + 1, right), (0, 0)))
    cs = jnp.cumsum(up, axis=1)
    total = cs[:, w:] - cs[:, :T]
    t = jnp.arange(T)
    cnt = (jnp.minimum(t + right, T - 1) - jnp.maximum(t - left, 0) + 1).astype(jnp.float32)
    return (total / cnt[None, :, None]).astype(u.dtype)


def pool_mixer(u, pool_w, pool_scale):
    B, T, _ = u.shape
    parts = []
    for g, w in enumerate(POOL_WINDOWS):
        ug = u[..., g * POOL_GROUP_DIM:(g + 1) * POOL_GROUP_DIM]
        parts.append(centred_window_mean(ug, w) - ug)
    p = jnp.stack(parts, axis=2)
    y = jnp.einsum("btgi,gio->btgo", p, pool_w).reshape(B, T, POOL_DIM)
    return y * pool_scale


def attention_latent(q_l, k_all, v_all):
    B, T = q_l.shape[:2]
    nb = T // Q_BLOCK
    scale = HEAD_DIM ** -0.5
    qb = q_l.reshape(B, nb, Q_BLOCK, N_KV_HEADS, Q_PER_KV, HEAD_DIM).transpose(1, 0, 2, 3, 4, 5)

    def block(q_blk):
        s = jnp.einsum("bqhgd,bkhd->bhgqk", q_blk, k_all).astype(jnp.float32) * scale
        p = jax.nn.softmax(s, axis=-1).astype(v_all.dtype)
        return jnp.einsum("bhgqk,bkhd->bqhgd", p, v_all)

    o = lax.map(block, qb)
    return o.transpose(1, 0, 2, 3, 4, 5).reshape(B, T, ATTN_DIM)


def attention_context(q_c, k_c, v_c):
    B, L = q_c.shape[:2]
    q = q_c.reshape(B, L, N_KV_HEADS, Q_PER_KV, HEAD_DIM)
    s = jnp.einsum("bqhgd,bkhd->bhgqk", q, k_c).astype(jnp.float32) * (HEAD_DIM ** -0.5)
    p = jax.nn.softmax(s, axis=-1).astype(v_c.dtype)
    return jnp.einsum("bhgqk,bkhd->bqhgd", p, v_c).reshape(B, L, ATTN_DIM)


def pool_attention_mixer(h_l, h_c, cos, sin, w_in, pool_w, pool_scale, q_gain, k_gain, w_out, need_ctx):
    def project(h):
        B, T, _ = h.shape
        u = h @ w_in
        a = u[..., :POOL_DIM]
        q = u[..., POOL_DIM:POOL_DIM + ATTN_DIM].reshape(B, T, N_Q_HEADS, HEAD_DIM)
        k = u[..., POOL_DIM + ATTN_DIM:POOL_DIM + ATTN_DIM + KV_DIM].reshape(B, T, N_KV_HEADS, HEAD_DIM)
        v = u[..., POOL_DIM + ATTN_DIM + KV_DIM:].reshape(B, T, N_KV_HEADS, HEAD_DIM)
        return a, rms_norm(q, q_gain), rms_norm(k, k_gain), v

    a_l, q_l, k_l, v_l = project(h_l)
    q_l = apply_rope(q_l, cos, sin)
    k_l = apply_rope(k_l, cos, sin)
    a_c, q_c, k_c, v_c = project(h_c)
    k_all = jnp.concatenate([k_c, k_l], axis=1)
    v_all = jnp.concatenate([v_c, v_l], axis=1)
    o_l = attention_latent(q_l, k_all, v_all)
    y_l = jnp.concatenate([pool_mixer(a_l, pool_w, pool_scale), o_l], axis=-1) @ w_out
    y_c = None
    if need_ctx:
        o_c = attention_context(q_c, k_c, v_c)
        y_c = jnp.concatenate([pool_mixer(a_c, pool_w, pool_scale), o_c], axis=-1) @ w_out
    return y_l, y_c


def ssd_scan(xs, dt, A, Bm, Cm, h0):
    Bsz, T = xs.shape[:2]
    nc = T // SSD_CHUNK
    Q = SSD_CHUNK
    xdt = (xs.astype(jnp.float32) * dt[..., None]).reshape(Bsz, nc, Q, SSM_GROUPS, HEADS_PER_GROUP, SSM_HEAD_DIM)
    a = (dt * A).reshape(Bsz, nc, Q, SSM_GROUPS, HEADS_PER_GROUP)
    a_cum = jnp.cumsum(a, axis=2)
    Bc = Bm.astype(jnp.float32).reshape(Bsz, nc, Q, SSM_GROUPS, D_STATE)
    Cc = Cm.astype(jnp.float32).reshape(Bsz, nc, Q, SSM_GROUPS, D_STATE)
    lower = jnp.tril(jnp.ones((Q, Q), dtype=bool))[None, :, :, None, None]

    def chunk_step(h, inp):
        xq, aq, Bq, Cq = inp
        seg = aq[:, :, None] - aq[:, None, :]
        Lmat = jnp.exp(jnp.where(lower, seg, -jnp.inf))
        cb = jnp.einsum("blgn,bsgn->blsg", Cq, Bq)
        y = (jnp.einsum("blsg,blsgh,bsghp->blghp", cb, Lmat, xq)
             + jnp.einsum("blgn,bghpn,blgh->blghp", Cq, h, jnp.exp(aq)))
        decay_to_end = jnp.exp(aq[:, -1:] - aq)
        h_new = (jnp.exp(aq[:, -1])[..., None, None] * h
                 + jnp.einsum("bsgn,bsgh,bsghp->bghpn", Bq, decay_to_end, xq))
        return h_new, y

    inputs = (jnp.swapaxes(xdt, 0, 1), jnp.swapaxes(a_cum, 0, 1), jnp.swapaxes(Bc, 0, 1), jnp.swapaxes(Cc, 0, 1))
    h_final, ys = lax.scan(chunk_step, h0, inputs)
    y = jnp.swapaxes(ys, 0, 1).reshape(Bsz, T, SSM_HEADS, SSM_HEAD_DIM)
    return y.astype(xs.dtype), h_final


def bidirectional_ssd_mixer(h_l, h_c, w_in, conv_w, conv_b, A_log, dt_bias, D_skip, norm_w, w_out, need_ctx):
    A = -jnp.exp(A_log.astype(jnp.float32))

    def prep(h):
        B, T, _ = h.shape
        u = h @ w_in
        z = u[..., :D_INNER]
        xbc = jax.nn.silu(depthwise_conv_centred(u[..., D_INNER:D_INNER + CONV_DIM], conv_w, conv_b))
        dt_raw = u[..., D_INNER + CONV_DIM:].reshape(B, T, 2, SSM_HEADS)
        xs = xbc[..., :D_INNER].reshape(B, T, SSM_HEADS, SSM_HEAD_DIM)
        Bm = xbc[..., D_INNER:D_INNER + SSM_GROUPS * D_STATE].reshape(B, T, SSM_GROUPS, D_STATE)
        Cm = xbc[..., D_INNER + SSM_GROUPS * D_STATE:].reshape(B, T, SSM_GROUPS, D_STATE)
        dt = jax.nn.softplus(dt_raw.astype(jnp.float32) + dt_bias.astype(jnp.float32))
        return z, xs, Bm, Cm, dt

    def run(z, xs, Bm, Cm, dt, h0_f, h0_b):
        fl = lambda t: jnp.flip(t, axis=1)
        y_f, hf = ssd_scan(xs, dt[:, :, 0], A[0], Bm, Cm, h0_f)
        y_b, hb = ssd_scan(fl(xs), fl(dt[:, :, 1]), A[1], fl(Bm), fl(Cm), h0_b)
        y = y_f + fl(y_b) + D_skip[:, None] * xs
        B, T = xs.shape[:2]
        y = rms_norm(y.reshape(B, T, D_INNER) * jax.nn.silu(z), norm_w)
        return y @ w_out, hf, hb

    zc, xc, Bc, Cc, dtc = prep(h_c)
    zeros = jnp.zeros((h_c.shape[0], SSM_GROUPS, HEADS_PER_GROUP, SSM_HEAD_DIM, D_STATE), jnp.float32)
    y_c = None
    if need_ctx:
        y_c, hc_f, hc_b = run(zc, xc, Bc, Cc, dtc, zeros, zeros)
    else:
        fl = lambda t: jnp.flip(t, axis=1)
        _, hc_f = ssd_scan(xc, dtc[:, :, 0], A[0], Bc, Cc, zeros)
        _, hc_b = ssd_scan(fl(xc), fl(dtc[:, :, 1]), A[1], fl(Bc), fl(Cc), zeros)
    zl, xl, Bl, Cl, dtl = prep(h_l)
    y_l, _, _ = run(zl, xl, Bl, Cl, dtl, hc_f, hc_b)
    return y_l, y_c


def conv_ffn(h, w_up, conv_w, conv_b, w_down):
    u = depthwise_conv_centred(h @ w_up, conv_w, conv_b)
    val = u[..., :D_FF]
    gate = u[..., D_FF:]
    return (jax.nn.silu(gate) * val) @ w_down


def setup_inputs(seed: int = 0) -> dict:
    key = jax.random.key(seed)
    ks = jax.random.split(key, 28)
    f32 = jnp.float32
    nrm = lambda k, s: jax.random.normal(k, s, f32)
    dense = lambda k, s, fan_in: nrm(k, s) * fan_in ** -0.5
    gain = lambda k, s: 1.0 + 0.05 * nrm(k, s)
    dt0 = jnp.exp(jax.random.uniform(ks[17], (N_ODD, 2, SSM_HEADS), f32, np.log(1e-3), np.log(1e-1)))
    return {
        "x": nrm(ks[0], (BATCH, SEQ, D_MODEL)),
        "c": nrm(ks[1], (BATCH, D_MODEL)),
        "ctx": nrm(ks[2], (BATCH, CTX_LEN, D_MODEL)),
        "c_ctx": nrm(ks[3], (D_MODEL,)),
        "ada_w": 0.5 * dense(ks[4], (DEPTH, D_MODEL, 6 * D_MODEL), D_MODEL),
        "ada_b": 0.02 * nrm(ks[5], (DEPTH, 6 * D_MODEL)),
        "norm_w": gain(ks[6], (DEPTH, 4, D_MODEL)),
        "attn_w_in": dense(ks[7], (N_EVEN, D_MODEL, IN_A_WIDTH), D_MODEL),
        "pool_w": dense(ks[8], (N_EVEN, POOL_GROUPS, POOL_GROUP_DIM, POOL_GROUP_DIM), POOL_GROUP_DIM),
        "pool_scale": gain(ks[9], (N_EVEN, POOL_DIM)),
        "q_gain": gain(ks[10], (N_EVEN, HEAD_DIM)),
        "k_gain": gain(ks[11], (N_EVEN, HEAD_DIM)),
        "attn_w_out": dense(ks[12], (N_EVEN, MIX_WIDTH, D_MODEL), MIX_WIDTH),
        "ssm_w_in": dense(ks[13], (N_ODD, D_MODEL, IN_C_WIDTH), D_MODEL),
        "ssm_conv_w": dense(ks[14], (N_ODD, CONV_DIM, D_CONV), D_CONV),
        "ssm_conv_b": 0.02 * nrm(ks[15], (N_ODD, CONV_DIM)),
        "ssm_A_log": jnp.log(jax.random.uniform(ks[16], (N_ODD, 2, SSM_HEADS), f32, 1.0, 16.0)),
        "ssm_dt_bias": dt0 + jnp.log(-jnp.expm1(-dt0)),
        "ssm_D": gain(ks[18], (N_ODD, SSM_HEADS)),
        "ssm_norm_w": gain(ks[19], (N_ODD, D_INNER)),
        "ssm_w_out": dense(ks[20], (N_ODD, D_INNER, D_MODEL), D_INNER),
        "ffn_w_up": dense(ks[21], (DEPTH, D_MODEL, 2 * D_FF), D_MODEL),
        "ffn_conv_w": dense(ks[22], (DEPTH, 2 * D_FF, FFN_CONV), FFN_CONV),
        "ffn_conv_b": 0.02 * nrm(ks[23], (DEPTH, 2 * D_FF)),
        "ffn_w_down": dense(ks[24], (DEPTH, D_FF, D_MODEL), D_FF),
    }


def reference(x, c, ctx, c_ctx, ada_w, ada_b, norm_w, attn_w_in, pool_w, pool_scale, q_gain, k_gain,
              attn_w_out, ssm_w_in, ssm_conv_w, ssm_conv_b, ssm_A_log, ssm_dt_bias, ssm_D, ssm_norm_w,
              ssm_w_out, ffn_w_up, ffn_conv_w, ffn_conv_b, ffn_w_down):
    T = x.shape[1]
    cos, sin = axial_rope_tables(T)
    for i in range(DEPTH):
        last = i == DEPTH - 1
        j = i // 2
        mod_l = jnp.split(adaln(c, ada_w[i], ada_b[i])[:, None, :], 6, axis=-1)
        mod_c = jnp.split(adaln(c_ctx, ada_w[i], ada_b[i])[None, None, :], 6, axis=-1)
        h_l = modulate(rms_norm(x, norm_w[i, 0]), mod_l[0], mod_l[1])
        h_c = modulate(rms_norm(ctx, norm_w[i, 0]), mod_c[0], mod_c[1])
        if i % 2 == 0:
            y_l, y_c = pool_attention_mixer(h_l, h_c, cos, sin, attn_w_in[j], pool_w[j], pool_scale[j],
                                            q_gain[j], k_gain[j], attn_w_out[j], not last)
        else:
            y_l, y_c = bidirectional_ssd_mixer(h_l, h_c, ssm_w_in[j], ssm_conv_w[j], ssm_conv_b[j], ssm_A_log[j],
                                               ssm_dt_bias[j], ssm_D[j], ssm_norm_w[j], ssm_w_out[j], not last)
        x = x + mod_l[2] * rms_norm(y_l, norm_w[i, 1])
        f_l = conv_ffn(modulate(rms_norm(x, norm_w[i, 2]), mod_l[3], mod_l[4]),
                       ffn_w_up[i], ffn_conv_w[i], ffn_conv_b[i], ffn_w_down[i])
        x = x + mod_l[5] * rms_norm(f_l, norm_w[i, 3])
        if not last:
            ctx = ctx + mod_c[2] * rms_norm(y_c, norm_w[i, 1])
            f_c = conv_ffn(modulate(rms_norm(ctx, norm_w[i, 2]), mod_c[3], mod_c[4]),
                           ffn_w_up[i], ffn_conv_w[i], ffn_conv_b[i], ffn_w_down[i])
            ctx = ctx + mod_c[5] * rms_norm(f_c, norm_w[i, 3])
    return x
```

```python
import numpy as np
from contextlib import ExitStack
import concourse.bass as bass
import concourse.mybir as mybir
from concourse.bass_utils import run_bass_kernel_spmd

F32 = mybir.dt.float32
BF16 = mybir.dt.bfloat16
AF = mybir.ActivationFunctionType
ALU = mybir.AluOpType
AX = mybir.AxisListType

ENGS = ("pe", "act", "dve", "pool", "sp")
NSLOT = 8
import os
USE_POOL = os.environ.get("USE_POOL", "0") == "1"

D = 1024
T = 8192
L = 256
NTOK = T + L
NT = NTOK // 128
EPS = 1e-6
DFF = 2816
NFF = DFF // 128
POOL_WINDOWS = (2, 4, 8, 16)


class Buf:
    __slots__ = ("name", "w", "r")

    def __init__(self, name=""):
        self.name = name
        self.w = None
        self.r = []


class TL:
    def __init__(self, t, name="", excl=False):
        self.t = t
        self.b = Buf(name)
        self.excl = excl

    def __getitem__(self, idx):
        return self.t[idx]


def _b(x):
    return x.b if isinstance(x, TL) else x


class Sched:
    LIMIT = 16000
    DLIMIT = 1000

    def __init__(self, nc, stack, strict=True):
        self.nc = nc
        self.stack = stack
        self.strict = strict
        self.ops = {e: [] for e in ENGS}
        self.cnt = {e: 0 for e in ENGS}
        self.tot = {e: 0 for e in ENGS}
        self.sems = {e: [stack.enter_context(nc.semaphore("s_" + e + "0"))] for e in ENGS}
        self.dq = ("sp", "act", "pool")
        self.dsems = {q: [[stack.enter_context(nc.semaphore(f"d_{q}_0_{i}")) for i in range(NSLOT)]]
                      for q in self.dq}
        self.dcnt = {q: 0 for q in self.dq}
        self.seen = {e: {} for e in ENGS}
        self.ninst = 0
        self.marks = []
        self.last_mark_n = 0

    def _ev(self, ev):
        if ev[0] == "e":
            _, eng, gen, val = ev
            return ("e", eng, gen), val, self.sems[eng][gen]
        _, q, gen, slot, val = ev
        return ("d", q, gen, slot), val, self.dsems[q][gen][slot]

    def _deps(self, eng, reads, writes):
        evs = []
        for b in reads:
            b = _b(b)
            if b.w is not None:
                evs.append(b.w)
        for b in writes:
            b = _b(b)
            if b.w is not None:
                evs.append(b.w)
            evs.extend(b.r)
        need = {}
        for ev in evs:
            if ev[0] == "e" and ev[1] == eng:
                if eng in ("pe", "sp") or not self.strict:
                    continue
            key, val, sem = self._ev(ev)
            if self.seen[eng].get(key, 0) >= val:
                continue
            if key not in need or need[key][0] < val:
                need[key] = (val, sem)
        waits = []
        for key, (val, sem) in need.items():
            self.seen[eng][key] = val
            waits.append((sem, val))
        return waits

    def _commit(self, ev, reads, writes):
        for b in reads:
            _b(b).r.append(ev)
        for b in writes:
            b = _b(b)
            b.w = ev
            b.r = []

    def op(self, eng, fn, reads=(), writes=()):
        if eng == "pool" and not USE_POOL:
            eng = "dve"
        ex = [b for b in reads if getattr(b, "excl", False)]
        if ex:
            reads = [b for b in reads if not getattr(b, "excl", False)]
            writes = list(writes) + ex
        waits = self._deps(eng, reads, writes)
        if self.cnt[eng] >= self.LIMIT:
            g = len(self.sems[eng])
            self.sems[eng].append(self.stack.enter_context(self.nc.semaphore(f"s_{eng}{g}")))
            self.cnt[eng] = 0
        self.cnt[eng] += 1
        self.tot[eng] += 1
        idx = self.cnt[eng]
        gen = len(self.sems[eng]) - 1
        sem = self.sems[eng][gen]

        def closure(e, waits=waits, fn=fn, sem=sem):
            for s, v in waits:
                e.wait_ge(s, v)
            fn(e).then_inc(sem, 1)

        self.ops[eng].append(closure)
        self._commit(("e", eng, gen, idx), reads, writes)
        self.ninst += 1

    def dma(self, q, out, in_, reads=(), writes=(), **kw):
        waits = self._deps(q, reads, writes)
        m = self.dcnt[q]
        self.dcnt[q] += 1
        per_gen = NSLOT * self.DLIMIT
        gen = m // per_gen
        mm_ = m % per_gen
        if gen >= len(self.dsems[q]):
            self.dsems[q].append([self.stack.enter_context(self.nc.semaphore(f"d_{q}_{gen}_{i}"))
                                  for i in range(NSLOT)])
        slot = mm_ % NSLOT
        sem = self.dsems[q][gen][slot]
        prev = 16 * (mm_ // NSLOT)
        key = ("d", q, gen, slot)
        if prev > 0 and self.seen[q].get(key, 0) < prev:
            waits.append((sem, prev))
            self.seen[q][key] = prev
        if prev == 0 and gen > 0:
            pass

        def closure(e, waits=waits, sem=sem, out=out, in_=in_, kw=kw):
            for s, v in waits:
                e.wait_ge(s, v)
            e.dma_start(out=out, in_=in_, **kw).then_inc(sem, 16)

        self.ops[q].append(closure)
        self._commit(("d", q, gen, slot, prev + 16), reads, writes)
        self.ninst += 1

    def _all_targets(self):
        tg = []
        for e in ENGS:
            for gen, sem in enumerate(self.sems[e]):
                val = self.LIMIT if gen < len(self.sems[e]) - 1 else self.cnt[e]
                if val > 0:
                    tg.append((("e", e, gen), val, sem))
        per_gen = NSLOT * self.DLIMIT
        for q in self.dq:
            for gen, sl in enumerate(self.dsems[q]):
                n_in_gen = min(per_gen, self.dcnt[q] - gen * per_gen)
                for slot in range(NSLOT):
                    n = (n_in_gen - slot + NSLOT - 1) // NSLOT
                    if n > 0:
                        tg.append((("d", q, gen, slot), 16 * n, sl[slot]))
        return tg

    def barrier(self, engines=ENGS):
        tg = self._all_targets()
        for eng in engines:
            waits = []
            for key, val, sem in tg:
                if key[0] == "e" and key[1] == eng and eng in ("pe", "sp"):
                    continue
                if self.seen[eng].get(key, 0) >= val:
                    continue
                self.seen[eng][key] = val
                waits.append((sem, val))
            if not waits:
                continue

            def closure(e, waits=waits):
                for s, v in waits:
                    e.wait_ge(s, v)

            self.ops[eng].append(closure)

    def finish(self):
        self.barrier(engines=("sp",))

    def mark(self, min_ops=0):
        if self.ninst - self.last_mark_n < min_ops:
            return
        self.last_mark_n = self.ninst
        self.marks.append({e: len(self.ops[e]) for e in ENGS})

    def emit(self):
        nc = self.nc
        bounds = self.marks + [{e: len(self.ops[e]) for e in ENGS}]
        prev = {e: 0 for e in ENGS}
        for bd in bounds:
            seg = {e: self.ops[e][prev[e]:bd[e]] for e in ENGS}
            prev = bd
            if not any(seg[e] for e in ENGS):
                continue
            with nc.Block() as block:
                @block.tensor
                def _(e, seg=seg):
                    for c in seg["pe"]:
                        c(e)

                @block.scalar
                def _(e, seg=seg):
                    for c in seg["act"]:
                        c(e)

                @block.vector
                def _(e, seg=seg):
                    for c in seg["dve"]:
                        c(e)

                @block.gpsimd
                def _(e, seg=seg):
                    for c in seg["pool"]:
                        c(e)

                @block.sync
                def _(e, seg=seg):
                    for c in seg["sp"]:
                        c(e)


def _fm(v, nchunk):
    return np.ascontiguousarray(np.asarray(v, np.float32).reshape(nchunk, 128).T)


def _pool_bands():
    out = np.zeros((4, 5, 128, 128), np.float32)
    n = 384
    for g, w in enumerate(POOL_WINDOWS):
        left = w // 2
        right = w - 1 - left
        M = np.zeros((n, n), np.float64)
        for t in range(n):
            lo = max(t - left, 0)
            hi = min(t + right, n - 1)
            M[lo:hi + 1, t] = 1.0 / (hi - lo + 1)
            M[t, t] -= 1.0
        out[g, 0] = M[0:128, 128:256]
        out[g, 1] = M[128:256, 128:256]
        out[g, 2] = M[256:384, 128:256]
        out[g, 3] = M[0:128, 0:128]
        out[g, 4] = M[256:384, 256:384]
    return out


def _rope_tables():
    rows = T // 64
    row = np.repeat(np.arange(rows, dtype=np.float32), 64)
    col = np.tile(np.arange(64, dtype=np.float32), rows)
    half = 64
    inv = (np.float32(10000.0) ** (-np.arange(0, half, 2, dtype=np.float32) / np.float32(half))).astype(np.float32)
    ang = np.concatenate([row[:, None] * inv, col[:, None] * inv], axis=-1).astype(np.float32)
    return np.cos(ang).astype(np.float32), np.sin(ang).astype(np.float32)


class MK:
    def __init__(self, phases=("p0", "l0", "f0", "l1", "f1"), dbg=(), nblk=None, nt1=None):
        self.phases = phases
        self.dbg = dbg
        self.nblk = nblk
        self.nt1 = nt1
        self.nc = bass.Bass("TRN2", target_bir_lowering=False)
        self.din = {}
        self.dout = {}

    def inp(self, name, shape, dt=F32):
        ap = self.nc.dram_tensor(name, list(shape), dt, kind="ExternalInput").ap()
        self.din[name] = ap
        return ap

    def scratch(self, name, shape, dt=F32):
        kind = "ExternalOutput" if name in self.dbg else "Internal"
        ap = self.nc.dram_tensor(name, list(shape), dt, kind=kind).ap()
        return ap

    def _uname(self, name):
        self._uid = getattr(self, "_uid", 0) + 1
        return f"{name}_u{self._uid}"

    def sb(self, st, name, shape, dt=F32):
        return TL(st.enter_context(self.nc.sbuf_tensor(self._uname("s_" + name), list(shape), dt)), name)

    def ps(self, st, name, shape, dt=F32):
        nbytes = int(np.prod(shape[1:])) * (2 if dt == BF16 else 4)
        assert nbytes % 2048 == 0, (name, shape)
        return TL(st.enter_context(self.nc.psum_tensor(self._uname("p_" + name), list(shape), dt)), name, excl=True)

    def act(self, out, in_, func, r, w, **kw):
        self.S.op("act", lambda e: e.activation(out=out, in_=in_, func=func, **kw), r, w)

    def ts(self, eng, out, in0, s1, s2, op0, op1, r, w):
        if op1 is None:
            self.S.op(eng, lambda e: e.tensor_scalar(out=out, in0=in0, scalar1=s1, scalar2=None, op0=op0), r, w)
        else:
            self.S.op(eng, lambda e: e.tensor_scalar(out=out, in0=in0, scalar1=s1, scalar2=s2, op0=op0, op1=op1), r, w)

    def tt(self, eng, out, in0, in1, op, r, w):
        self.S.op(eng, lambda e: e.tensor_tensor(out=out, in0=in0, in1=in1, op=op), r, w)

    def stt(self, out, in0, scalar, in1, op0, op1, r, w):
        self.S.op("dve", lambda e: e.scalar_tensor_tensor(out=out, in0=in0, scalar=scalar, in1=in1,
                                                          op0=op0, op1=op1), r, w)

    def cp(self, eng, out, in_, r, w):
        if eng == "act":
            self.S.op("act", lambda e: e.copy(out=out, in_=in_), r, w)
        else:
            self.S.op(eng, lambda e: e.tensor_copy(out=out, in_=in_), r, w)

    def memset(self, eng, ap, val, w):
        self.S.op(eng, lambda e: e.memset(ap, val), (), w)

    def mm(self, items, r, w):
        def fn(e, items=items):
            ins = None
            for (o, l, rr, s0, s1) in items:
                ins = e.matmul(o, lhsT=l, rhs=rr, start=s0, stop=s1)
            return ins
        self.S.op("pe", fn, r, w)

    def trn(self, items, r, w):
        def fn(e, items=items):
            ins = None
            for (o, i, idn) in items:
                ins = e.transpose(out=o, in_=i, identity=idn)
            return ins
        self.S.op("pe", fn, r, w)

    def rstd(self, src, n, junk, ss, r, w_extra=()):
        self.act(junk, src, AF.Square, r, [ss] + list(w_extra), accum_out=ss[:, 0:1])
        self.ts("dve", ss[:, 0:1], ss[:, 0:1], 1.0 / n, EPS, ALU.mult, ALU.add, [ss], [ss])
        self.act(ss[:, 0:1], ss[:, 0:1], AF.Sqrt, [ss], [ss])
        self.S.op("dve", lambda e: e.reciprocal(out=ss[:, 0:1], in_=ss[:, 0:1]), [ss], [ss])

    def build(self):
        nc = self.nc
        inp = self.inp
        self.xa = inp("xa", [NTOK, D])
        self.cond = inp("cond", [128, 8, 2])
        self.ada_w = inp("ada_w", [2, D, 6 * D])
        self.ada_b = inp("ada_b", [2, 128, 48])
        self.nw = inp("nw", [128, 2, 4, 8])
        self.ident = inp("ident", [128, 128])
        self.w_in0 = inp("w_in0", [D, 1536])
        self.w_out0 = inp("w_out0", [D, D])
        self.pwblk = inp("pwblk", [128, 2, 128])
        self.pscale = inp("pscale", [128, 2])
        self.qkg = inp("qkg", [128, 2, 128])
        self.bands = inp("bands", [128, 20, 128])
        self.cos = inp("cos", [T, 64])
        self.sin = inp("sin", [T, 64])
        self.w_up = inp("w_up", [2, D, 2 * DFF])
        self.w_dn = inp("w_dn", [2, DFF, D])
        self.fcw = inp("fcw", [2, 128, 2 * NFF, 3])
        self.fcb = inp("fcb", [2, 128, 2 * NFF])

        self.w_in1 = inp("w_in1", [D, 5184])
        self.w_out1 = inp("w_out1", [2048, D])
        self.scw = inp("scw", [128, 24, 4])
        self.scb = inp("scb", [128, 24])
        self.dtb = inp("dtb", [128, 64])
        self.alog = inp("alog", [128, 64])
        self.dsk = inp("dsk", [128, 32])
        self.snw = inp("snw", [128, 16])
        self.msk = inp("msk", [128, 4, 128])
        self.rec_tok = self.scratch("rec_tok", [NT, 128, 4608], BF16)
        self.rec_fm = self.scratch("rec_fm", [NT, 128, 8, 128], BF16)
        self.dt_d = self.scratch("dt_d", [NT, 128, 64])
        self.yb_d = self.scratch("yb_d", [NT, 128, 2048])
        self.xm1 = self.scratch("xm1", [NTOK, D])
        self.b_rec = [Buf() for _ in range(NT)]
        self.b_yb = [Buf() for _ in range(NT)]
        self.b_xm1 = [Buf() for _ in range(NT)]
        self.b_out = [Buf() for _ in range(NT)]
        self.xm0 = self.scratch("xm0", [NTOK, D])
        self.xb = self.scratch("xb", [NTOK, D])
        self.q_d = self.scratch("q_d", [NT, 128, 6, 128], BF16)
        self.yp_d = self.scratch("yp_d", [NT, 128, 2, 128], BF16)
        final_name = {"f0": "xb", "l0": "xm0", "l1": "xm1"}.get(self.phases[-1], "out")
        self.final_name = final_name
        if final_name == "out":
            self.out = nc.dram_tensor("out", [T, D], F32, kind="ExternalOutput").ap()

        self.b_xa = [Buf() for _ in range(NT)]
        self.b_xm0 = [Buf() for _ in range(NT)]
        self.b_xb = [Buf() for _ in range(NT)]
        self.b_q = [Buf() for _ in range(NT)]
        self.b_yp = [Buf() for _ in range(NT)]

        with ExitStack() as st:
            self.S = Sched(nc, st)
            self.glob(st)
            for ph in self.phases:
                getattr(self, "ph_" + ph)()
                self.S.barrier()
                self.S.mark()
            self.S.finish()
            self.S.emit()
        return nc

    def glob(self, st):
        sb = lambda n, s, d=F32: self.sb(st, n, s, d)
        self.idf = sb("idf", [128, 128])
        self.idb = sb("idb", [128, 128], BF16)
        self.ones = sb("ones", [128, 128])
        self.mod = [sb(f"mod{i}", [128, 48, 2]) for i in range(2)]
        self.nws = sb("nws", [128, 2, 4, 8])
        self.A1 = [[sb(f"A1_{i}{s}", [128, 8]) for s in range(2)] for i in range(2)]
        self.A2 = [[sb(f"A2_{i}{s}", [128, 8]) for s in range(2)] for i in range(2)]
        self.G1 = [[sb(f"G1_{i}{s}", [128, 8]) for s in range(2)] for i in range(2)]
        self.G2 = [[sb(f"G2_{i}{s}", [128, 8]) for s in range(2)] for i in range(2)]
        S = self.S
        S.dma("sp", self.idf[:], self.ident, writes=[self.idf])
        S.dma("sp", self.nws[:], self.nw, writes=[self.nws])
        self.cp("dve", self.idb[:], self.idf[:], [self.idf], [self.idb])
        self.memset("dve", self.ones[:], 1.0, [self.ones])

    def ph_p0(self):
        S = self.S
        with ExitStack() as st:
            sb = lambda n, s, d=F32: self.sb(st, n, s, d)
            cond = sb("cond", [128, 8, 2])
            scond = sb("scond", [128, 8, 2])
            adab = sb("adab", [128, 48])
            wst = [sb(f"wst{j}", [128, 8, 768]) for j in range(2)]
            pm = self.ps(st, "pm", [128, 256, 2])
            S.dma("sp", cond[:], self.cond, writes=[cond])
            self.act(scond[:], cond[:], AF.Silu, [cond], [scond])
            for i in range(2):
                S.dma("sp", adab[:], self.ada_b[i], writes=[adab])
                for cg in range(8):
                    w = wst[cg % 2]
                    S.dma("sp", w[:], self.ada_w[i][:, cg * 768:(cg + 1) * 768].rearrange("(k p) n -> p k n", p=128),
                          writes=[w])
                    for cc in range(6):
                        c = cg * 6 + cc
                        items = [(pm[:, c, :], w[:, k, cc * 128:(cc + 1) * 128], scond[:, k, :], k == 0, k == 7)
                                 for k in range(8)]
                        self.mm(items, [w, scond], [pm])
                md = self.mod[i]
                if "modd" in self.dbg:
                    if i == 0:
                        self.modd = self.scratch("modd", [2, 128, 96])
                self.tt("dve", md[:], pm[:, 0:48, :], adab[:].unsqueeze(2).to_broadcast([128, 48, 2]), ALU.add,
                        [pm, adab], [md])
                for s in range(2):
                    nws = self.nws
                    self.stt(self.A1[i][s][:], md[:, 8:16, s], 1.0, nws[:, i, 0, :], ALU.add, ALU.mult,
                             [md, nws], [self.A1[i][s]])
                    self.stt(self.A2[i][s][:], md[:, 32:40, s], 1.0, nws[:, i, 2, :], ALU.add, ALU.mult,
                             [md, nws], [self.A2[i][s]])
                    self.tt("dve", self.G1[i][s][:], md[:, 16:24, s], nws[:, i, 1, :], ALU.mult,
                            [md, nws], [self.G1[i][s]])
                    self.tt("dve", self.G2[i][s][:], md[:, 40:48, s], nws[:, i, 3, :], ALU.mult,
                            [md, nws], [self.G2[i][s]])
                if "modd" in self.dbg:
                    S.dma("sp", self.modd[i], md[:].rearrange("p a b -> p (a b)"), reads=[md], writes=[Buf()])

    def bcast_rows(self, st_tmp, pB, srcs, dsts):
        tmp = [self.sb(st_tmp, f"bct{j}", [128, 128]) for j in range(2)]
        n = 0
        for src, dst in zip(srcs, dsts):
            for k in range(8):
                t = tmp[n % 2]
                n += 1
                self.ts("dve", t[:], self.ones[:], src[:, k:k + 1], None, ALU.mult, None, [self.ones, src], [t])
                self.mm([(pB[:, k * 128:(k + 1) * 128], t[:], self.idf[:], True, True)], [t, self.idf], [pB])
            self.cp("act", dst[:], pB[:], [pB], [dst])

    def load_w_bf16(self, dst, src_dram, kchunks, ncols, step=512):
        for c0 in range(0, ncols, step):
            c1 = min(ncols, c0 + step)
            for k0 in range(0, kchunks, 8):
                k1 = min(kchunks, k0 + 8)
                self.S.dma("pool", dst[:, k0:k1, c0:c1],
                           src_dram[k0 * 128:k1 * 128, c0:c1].rearrange("(k p) n -> p k n", p=128),
                           writes=[dst])

    def norm_tile(self, xt, st_, A, B, hT_out, hT_tl, col0, scr):
        (sq, ss, xn, pT) = scr
        self.rstd(xt[:], D, sq[:], ss, [xt], [sq])
        self.ts("dve", xn[:], xt[:], ss[:, 0:1], None, ALU.mult, None, [xt, ss], [xn])
        self.trn([(pT[:, k, :], xn[:, k * 128:(k + 1) * 128], self.idb[:]) for k in range(8)],
                 [xn, self.idb], [pT])
        for k in range(8):
            self.act(hT_out[:, k, col0:col0 + 128], pT[:, k, :], AF.Identity, [pT, A, B], [hT_tl],
                     scale=A[:, k:k + 1], bias=B[:, k:k + 1])

    def ph_l0(self):
        S = self.S
        i_layer = 0
        with ExitStack() as st0:
            sb0 = lambda n, s, d=F32: self.sb(st0, n, s, d)
            kT = sb0("kT", [128, 2, NTOK], BF16)
            vA = sb0("vA", [128, NT, 2, 129], BF16)
            Gbc = [sb0(f"G1bc{s}", [128, D]) for s in range(2)]
            self.memset("pool", vA[:, :, :, 128:129], 1.0, [vA])
            with ExitStack() as st:
                sb = lambda n, s, d=F32: self.sb(st, n, s, d)
                ps = lambda n, s, d=F32: self.ps(st, n, s, d)
                w_in = sb("w_in", [128, 8, 1536], BF16)
                self.load_w_bf16(w_in, self.w_in0, 8, 1536)
                bands = sb("bands", [128, 20, 128])
                S.dma("sp", bands[:], self.bands, writes=[bands])
                pwb_f = sb("pwb_f", [128, 2, 128])
                pwb = sb("pwb", [128, 2, 128], BF16)
                S.dma("sp", pwb_f[:], self.pwblk, writes=[pwb_f])
                self.cp("dve", pwb[:], pwb_f[:], [pwb_f], [pwb])
                psc = sb("psc", [128, 2])
                S.dma("sp", psc[:], self.pscale, writes=[psc])
                qkg = sb("qkg", [128, 2, 128])
                S.dma("sp", qkg[:], self.qkg, writes=[qkg])
                xt = [sb(f"xt{j}", [128, D]) for j in range(2)]
                cs = [sb(f"cs{j}", [128, 2, 64]) for j in range(2)]
                sq = sb("sq", [128, D], BF16)
                ss = sb("ss", [128, 1])
                xn = sb("xn", [128, D], BF16)
                hT = sb("hT", [128, 8, 128], BF16)
                a_sb = [sb(f"a{j}", [128, 256]) for j in range(4)]
                qsq = sb("qsq", [128, 8, 128], BF16)
                ssq = sb("ssq", [128, 8])
                qkn = sb("qkn", [128, 8, 128])
                t1 = sb("t1", [128, 8, 64]); t2 = sb("t2", [128, 8, 64])
                t3 = sb("t3", [128, 8, 64]); t4 = sb("t4", [128, 8, 64])
                qr = sb("qr", [128, 8, 128], BF16)
                qst = [sb(f"qst{j}", [128, 6, 128], BF16) for j in range(2)]
                p_sb = sb("p_sb", [128, 256])
                pT_sb = sb("pT_sb", [128, 2, 128], BF16)
                ypst = [sb(f"ypst{j}", [128, 2, 128], BF16) for j in range(2)]
                pT = ps("pT", [128, 8, 128], BF16)
                pu = [ps(f"pu{j}", [128, 512]) for j in range(3)]
                pQ = ps("pQ", [128, 8, 128], BF16)
                pp = ps("pp", [128, 512])
                ptr = ps("ptr", [128, 4, 128])
                py = ps("py", [128, 4, 128])

                def pool_tile(j):
                    s = 0 if j < 2 else 1
                    first = (j == 0) or (j == 2)
                    last = (j == 1) or (j == NT1 - 1)
                    for g in range(4):
                        items = []
                        nbs = []
                        if not first:
                            nbs.append((j - 1, 0))
                        nbs.append((j, 3 if first else (4 if last else 1)))
                        if not last:
                            nbs.append((j + 1, 2))
                        for n_i, (jj, kind) in enumerate(nbs):
                            items.append((pp[:, g * 64:(g + 1) * 64], bands[:, g * 5 + kind, :],
                                          a_sb[jj % 4][:, g * 64:(g + 1) * 64], n_i == 0, n_i == len(nbs) - 1))
                        self.mm(items, [bands] + [a_sb[jj % 4] for jj, _ in nbs], [pp])
                    self.cp("act", p_sb[:], pp[:, 0:256], [pp], [p_sb])
                    self.trn([(ptr[:, c, :], p_sb[:, c * 128:(c + 1) * 128], self.idf[:]) for c in range(2)],
                             [p_sb, self.idf], [ptr])
                    self.cp("dve", pT_sb[:], ptr[:, 0:2, :], [ptr], [pT_sb])
                    self.mm([(py[:, c, :], pwb[:, c, :], pT_sb[:, c, :], True, True) for c in range(2)],
                            [pwb, pT_sb], [py])
                    yp = ypst[j % 2]
                    for c in range(2):
                        self.act(yp[:, c, :], py[:, c, :], AF.Identity, [py, psc], [yp], scale=psc[:, c:c + 1])
                    S.dma("sp", self.yp_d[j], yp[:], reads=[yp], writes=[self.b_yp[j]])

                NT1 = self.nt1 or NT
                for i in range(NT1):
                    S.mark(4000)
                    s = 0 if i >= 2 else 1
                    x_t = xt[i % 2]
                    S.dma("sp", x_t[:], self.xa[i * 128:(i + 1) * 128, :], reads=[self.b_xa[i]], writes=[x_t])
                    if i >= 2:
                        c_t = cs[i % 2]
                        S.dma("sp", c_t[:, 0, :], self.cos[(i - 2) * 128:(i - 1) * 128, :], writes=[c_t])
                        S.dma("sp", c_t[:, 1, :], self.sin[(i - 2) * 128:(i - 1) * 128, :], writes=[c_t])
                    self.rstd(x_t[:], D, sq[:], ss, [x_t], [sq])
                    self.ts("dve", xn[:], x_t[:], ss[:, 0:1], None, ALU.mult, None, [x_t, ss], [xn])
                    self.trn([(pT[:, k, :], xn[:, k * 128:(k + 1) * 128], self.idb[:]) for k in range(8)],
                             [xn, self.idb], [pT])
                    A = self.A1[0][s]
                    md = self.mod[0]
                    for k in range(8):
                        self.act(hT[:, k, :], pT[:, k, :], AF.Identity, [pT, A, md], [hT],
                                 scale=A[:, k:k + 1], bias=md[:, k, s:s + 1])
                    for n in range(3):
                        self.mm([(pu[n][:], hT[:, k, :], w_in[:, k, n * 512:(n + 1) * 512], k == 0, k == 7)
                                 for k in range(8)], [hT, w_in], [pu[n]])
                    a_t = a_sb[i % 4]
                    self.cp("act", a_t[:], pu[0][:, 0:256], [pu[0]], [a_t])
                    self.cp("act", vA[:, i, :, 0:128], pu[2][:, 256:512].rearrange("p (h d) -> p h d", h=2),
                            [pu[2]], [vA])
                    pieces = [(pu[0][:, 256:512], 0, 2), (pu[1][:, 0:512], 2, 4), (pu[2][:, 0:256], 6, 2)]
                    for (src, h0, nh) in pieces:
                        self.act(qsq[:, h0:h0 + nh, :], src.rearrange("p (h d) -> p h d", h=nh), AF.Square,
                                 [pu[0], pu[1], pu[2]], [qsq])
                    self.S.op("dve", lambda e: e.tensor_reduce(out=ssq[:], in_=qsq[:], axis=AX.X, op=ALU.add),
                              [qsq], [ssq])
                    self.ts("dve", ssq[:], ssq[:], 1.0 / 128, EPS, ALU.mult, ALU.add, [ssq], [ssq])
                    self.act(ssq[:], ssq[:], AF.Sqrt, [ssq], [ssq])
                    self.S.op("dve", lambda e: e.reciprocal(out=ssq[:], in_=ssq[:]), [ssq], [ssq])
                    for (src, h0, nh) in pieces:
                        for hh in range(nh):
                            h = h0 + hh
                            gsel = 0 if h < 6 else 1
                            self.stt(qkn[:, h, :], src[:, hh * 128:(hh + 1) * 128], ssq[:, h:h + 1],
                                     qkg[:, gsel, :], ALU.mult, ALU.mult, [pu[0], pu[1], pu[2], ssq, qkg], [qkn])
                    if i >= 2:
                        c_t = cs[i % 2]
                        cosb = c_t[:, 0, :].unsqueeze(1).to_broadcast([128, 8, 64])
                        sinb = c_t[:, 1, :].unsqueeze(1).to_broadcast([128, 8, 64])
                        x1 = qkn[:, :, 0::2]
                        x2 = qkn[:, :, 1::2]
                        self.tt("dve", t1[:], x1, cosb, ALU.mult, [qkn, c_t], [t1])
                        self.tt("dve", t2[:], x2, sinb, ALU.mult, [qkn, c_t], [t2])
                        self.tt("dve", qr[:, :, 0::2], t1[:], t2[:], ALU.subtract, [t1, t2], [qr])
                        self.tt("pool", t3[:], x1, sinb, ALU.mult, [qkn, c_t], [t3])
                        self.tt("pool", t4[:], x2, cosb, ALU.mult, [qkn, c_t], [t4])
                        self.tt("pool", qr[:, :, 1::2], t3[:], t4[:], ALU.add, [t3, t4], [qr])
                    else:
                        self.cp("dve", qr[:], qkn[:], [qkn], [qr])
                    self.trn([(pQ[:, h, :], qr[:, h, :], self.idb[:]) for h in range(8)], [qr, self.idb], [pQ])
                    q_s = qst[i % 2]
                    self.cp("act", q_s[:], pQ[:, 0:6, :], [pQ], [q_s])
                    self.cp("dve", kT[:, :, i * 128:(i + 1) * 128], pQ[:, 6:8, :], [pQ], [kT])
                    S.dma("sp", self.q_d[i], q_s[:], reads=[q_s], writes=[self.b_q[i]])
                    if i == 1:
                        pool_tile(0)
                        pool_tile(1)
                    elif i >= 3:
                        pool_tile(i - 1)
                        if i == NT1 - 1:
                            pool_tile(i)
            S.barrier()
            with ExitStack() as st:
                sb = lambda n, s, d=F32: self.sb(st, n, s, d)
                ps = lambda n, s, d=F32: self.ps(st, n, s, d)
                wout = sb("wout", [128, 8, D], BF16)
                self.load_w_bf16(wout, self.w_out0, 8, D)
                qblk = [sb(f"qblk{j}", [128, 4, 6, 128], BF16) for j in range(2)]
                pex = [sb(f"pex{j}", [128, 512], BF16) for j in range(3)]
                o_sb = sb("o_sb", [128, 4, 768])
                rsum = sb("rsum", [128, 4])
                mixT = [sb(f"mixT{j}", [128, 8, 512], BF16) for j in range(2)]
                xres = [sb(f"xres{j}", [128, D]) for j in range(2)]
                tmp = sb("tmp", [128, D])
                xo = [sb(f"xo{j}", [128, D]) for j in range(2)]
                sq = sb("sq", [128, D], BF16)
                ss = sb("ss", [128, 1])
                pS = [ps(f"pS{j}", [128, 512]) for j in range(2)]
                pO = [ps(f"pO{j}", [128, 512]) for j in range(4)]
                pY = ps("pY", [128, D])
                self.bcast_rows(st, pY, [self.G1[0][0], self.G1[0][1]], Gbc)
                scale = 128.0 ** -0.5
                blocks = [([0, 1], [0, 1], 1)] + [([2 + 4 * b + j for j in range(4)], list(range(self.nt1 or NT)), 0)
                                                  for b in range(T // 512)]
                if self.nblk is not None:
                    blocks = blocks[:self.nblk]
                for bi, (qtiles, ktiles, s) in enumerate(blocks):
                    S.mark(4000)
                    nsub = len(qtiles)
                    nq = nsub * 128
                    qb = qblk[bi % 2]
                    mx = mixT[bi % 2]
                    for j, qt in enumerate(qtiles):
                        S.dma("sp", qb[:, j, :, :], self.q_d[qt], reads=[self.b_q[qt]], writes=[qb])
                        S.dma("sp", mx[:, 0:2, j * 128:(j + 1) * 128], self.yp_d[qt], reads=[self.b_yp[qt]],
                              writes=[mx])
                    steps = [(h, kt) for h in range(6) for kt in ktiles]
                    nstep = len(steps)

                    def qk(si):
                        h, kt = steps[si]
                        self.mm([(pS[si % 2][:, 0:nq].rearrange("p (a b) -> p a b", b=128), kT[:, h // 3, kt * 128:(kt + 1) * 128],
                                  qb[:, 0:nsub, h, :], True, True)], [kT, qb], [pS[si % 2]])
                    qk(0)
                    for si in range(nstep):
                        h, kt = steps[si]
                        if si + 1 < nstep:
                            qk(si + 1)
                        pe_ = pex[si % 3]
                        self.act(pe_[:, 0:nq], pS[si % 2][:, 0:nq], AF.Exp, [pS[si % 2]], [pe_], scale=scale)
                        first = kt == ktiles[0]
                        lastk = kt == ktiles[-1]
                        self.mm([(pO[sub][:, 0:129], pe_[:, sub * 128:(sub + 1) * 128], vA[:, kt, h // 3, :],
                                  first, lastk) for sub in range(nsub)], [pe_, vA], [pO[sub] for sub in range(nsub)])
                        if lastk:
                            for sub in range(nsub):
                                self.S.op("dve", lambda e, sub=sub: e.reciprocal(out=rsum[:, sub:sub + 1],
                                                                                 in_=pO[sub][:, 128:129]),
                                          [pO[sub]], [rsum])
                                self.ts("dve", o_sb[:, sub, h * 128:(h + 1) * 128], pO[sub][:, 0:128],
                                        rsum[:, sub:sub + 1], None, ALU.mult, None, [pO[sub], rsum], [o_sb])
                    for sub in range(nsub):
                        for hb in range(0, 6, 4):
                            hs = list(range(hb, min(6, hb + 4)))
                            pst = pS[(sub * 2 + hb // 4) % 2]
                            self.trn([(pst[:, jj * 128:(jj + 1) * 128], o_sb[:, sub, h * 128:(h + 1) * 128],
                                       self.idf[:]) for jj, h in enumerate(hs)], [o_sb, self.idf], [pst])
                            self.cp("act", mx[:, 2 + hb:2 + hb + len(hs), sub * 128:(sub + 1) * 128],
                                    pst[:, 0:len(hs) * 128].rearrange("p (h d) -> p h d", h=len(hs)), [pst], [mx])
                    for sub, qt in enumerate(qtiles):
                        xr = xres[sub % 2]
                        S.dma("sp", xr[:], self.xa[qt * 128:(qt + 1) * 128, :], reads=[self.b_xa[qt]], writes=[xr])
                        for n in range(2):
                            self.mm([(pY[:, n * 512:(n + 1) * 512], mx[:, k, sub * 128:(sub + 1) * 128],
                                      wout[:, k, n * 512:(n + 1) * 512], k == 0, k == 7) for k in range(8)],
                                    [mx, wout], [pY])
                        self.rstd(pY[:], D, sq[:], ss, [pY], [sq])
                        self.stt(tmp[:], pY[:], ss[:, 0:1], Gbc[s][:], ALU.mult, ALU.mult, [pY, ss, Gbc[s]], [tmp])
                        x_o = xo[sub % 2]
                        self.tt("pool", x_o[:], tmp[:], xr[:], ALU.add, [tmp, xr], [x_o])
                        S.dma("sp", self.xm0[qt * 128:(qt + 1) * 128, :], x_o[:], reads=[x_o],
                              writes=[self.b_xm0[qt]])

    def ffn(self, li, src, b_src, dst, b_dst, dst_off, tiles_blocks):
        S = self.S
        with ExitStack() as st:
            sb = lambda n, s, d=F32: self.sb(st, n, s, d)
            ps = lambda n, s, d=F32: self.ps(st, n, s, d)
            wup = sb("wup", [128, 8, 2 * DFF], BF16)
            wdn = sb("wdn", [128, NFF, D], BF16)
            self.load_w_bf16(wup, self.w_up[li], 8, 2 * DFF)
            self.load_w_bf16(wdn, self.w_dn[li], NFF, D)
            cw = sb("cw", [128, 2 * NFF, 3])
            cb = sb("cb", [128, 2 * NFF])
            S.dma("sp", cw[:], self.fcw[li], writes=[cw])
            S.dma("sp", cb[:], self.fcb[li], writes=[cb])
            streams = sorted(set(s for _, s in tiles_blocks))
            Gbc = {s: sb(f"G2bc{s}", [128, D]) for s in streams}
            hTb = [sb(f"hTb{j}", [128, 8, 258], BF16) for j in range(2)]
            xt = [sb(f"xt{j}", [128, D]) for j in range(2)]
            xres = [sb(f"xres{j}", [128, D]) for j in range(2)]
            sq = sb("sq", [128, D], BF16)
            ss = sb("ss", [128, 1])
            xn = sb("xn", [128, D], BF16)
            cv = [sb(f"cv{j}", [128, 256]) for j in range(4)]
            sg = sb("sg", [128, 256])
            gT = sb("gT", [128, NFF, 256], BF16)
            tmp = sb("tmp", [128, D])
            xo = [sb(f"xo{j}", [128, D]) for j in range(2)]
            pT = ps("pT", [128, 8, 128], BF16)
            pU = [ps(f"pU{j}", [128, 512]) for j in range(4)]
            pF = ps("pF", [128, D])
            self.bcast_rows(st, pF, [self.G2[li][s] for s in streams], [Gbc[s] for s in streams])
            nb = len(tiles_blocks)

            def stageA(j):
                t0, s = tiles_blocks[j]
                hb = hTb[j % 2]
                A = self.A2[li][s]
                md = self.mod[li]
                for tt_ in range(2):
                    ti = t0 + tt_
                    x_t = xt[tt_]
                    S.dma("sp", x_t[:], src[ti * 128:(ti + 1) * 128, :], reads=[b_src[ti]], writes=[x_t])
                    self.rstd(x_t[:], D, sq[:], ss, [x_t], [sq])
                    self.ts("dve", xn[:], x_t[:], ss[:, 0:1], None, ALU.mult, None, [x_t, ss], [xn])
                    self.trn([(pT[:, k, :], xn[:, k * 128:(k + 1) * 128], self.idb[:]) for k in range(8)],
                             [xn, self.idb], [pT])
                    for k in range(8):
                        self.act(hb[:, k, 1 + tt_ * 128:1 + (tt_ + 1) * 128], pT[:, k, :], AF.Identity,
                                 [pT, A, md], [hb], scale=A[:, k:k + 1], bias=md[:, 24 + k, s:s + 1])

            def halos(j):
                t0, s = tiles_blocks[j]
                hb = hTb[j % 2]
                cont = j > 0 and tiles_blocks[j - 1][1] == s and tiles_blocks[j - 1][0] + 2 == t0
                if cont:
                    hp = hTb[(j - 1) % 2]
                    self.cp("pool", hb[:, :, 0:1], hp[:, :, 256:257], [hp], [hb])
                    self.cp("pool", hp[:, :, 257:258], hb[:, :, 1:2], [hb], [hp])
                else:
                    self.memset("pool", hb[:, :, 0:1], 0.0, [hb])
                    if j > 0:
                        hp = hTb[(j - 1) % 2]
                        self.memset("pool", hp[:, :, 257:258], 0.0, [hp])

            def conv(pu_, c, t_a, t_b):
                self.act(t_a[:], pu_[:, 1:257], AF.Identity, [pu_, cw, cb], [t_a],
                         scale=cw[:, c, 1:2], bias=cb[:, c:c + 1])
                self.stt(t_b[:], pu_[:, 0:256], cw[:, c, 0:1], t_a[:], ALU.mult, ALU.add, [pu_, cw, t_a], [t_b])
                self.stt(t_a[:], pu_[:, 2:258], cw[:, c, 2:3], t_b[:], ALU.mult, ALU.add, [pu_, cw, t_b], [t_a])

            def stageB(j):
                t0, s = tiles_blocks[j]
                hb = hTb[j % 2]
                for cc in range(NFF):
                    puv = pU[(cc % 2) * 2]
                    pug = pU[(cc % 2) * 2 + 1]
                    self.mm([(puv[:, 0:258], wup[:, k, cc * 128:(cc + 1) * 128], hb[:, k, :], k == 0, k == 7)
                             for k in range(8)], [wup, hb], [puv])
                    self.mm([(pug[:, 0:258], wup[:, k, DFF + cc * 128:DFF + (cc + 1) * 128], hb[:, k, :],
                              k == 0, k == 7) for k in range(8)], [wup, hb], [pug])
                    conv(puv, cc, cv[0], cv[1])
                    conv(pug, NFF + cc, cv[2], cv[3])
                    self.act(sg[:], cv[2][:], AF.Silu, [cv[2]], [sg])
                    self.tt("pool", gT[:, cc, :], sg[:], cv[0][:], ALU.mult, [sg, cv[0]], [gT])
                for tt_ in range(2):
                    ti = t0 + tt_
                    xr = xres[tt_]
                    S.dma("sp", xr[:], src[ti * 128:(ti + 1) * 128, :], reads=[b_src[ti]], writes=[xr])
                    for n in range(2):
                        self.mm([(pF[:, n * 512:(n + 1) * 512], gT[:, cc, tt_ * 128:(tt_ + 1) * 128],
                                  wdn[:, cc, n * 512:(n + 1) * 512], cc == 0, cc == NFF - 1) for cc in range(NFF)],
                                [gT, wdn], [pF])
                    self.rstd(pF[:], D, sq[:], ss, [pF], [sq])
                    self.stt(tmp[:], pF[:], ss[:, 0:1], Gbc[s][:], ALU.mult, ALU.mult, [pF, ss, Gbc[s]], [tmp])
                    x_o = xo[tt_]
                    self.tt("pool", x_o[:], tmp[:], xr[:], ALU.add, [tmp, xr], [x_o])
                    r0 = ti * 128 - dst_off
                    S.dma("sp", dst[r0:r0 + 128, :], x_o[:], reads=[x_o], writes=[b_dst[ti]])

            stageA(0)
            halos(0)
            for j in range(nb):
                S.mark(4000)
                if j + 1 < nb:
                    stageA(j + 1)
                    halos(j + 1)
                else:
                    self.memset("pool", hTb[j % 2][:, :, 257:258], 0.0, [hTb[j % 2]])
                stageB(j)

    def ph_f0(self):
        nt1 = self.nt1 or NT
        blocks = [(0, 1)] + [(2 + 2 * j, 0) for j in range((nt1 - 2) // 2)]
        self.ffn(0, self.xm0, self.b_xm0, self.xb, self.b_xb, 0, blocks)


    def ph_f1(self):
        nt1 = self.nt1 or NT
        blocks = [(2 + 2 * j, 0) for j in range((nt1 - 2) // 2)]
        self.ffn(1, self.xm1, self.b_xm1, self.out, self.b_out, 256, blocks)

    def ph_l1(self):
        self.l1_prep()
        self.S.barrier()
        nt1 = self.nt1 or NT
        self.scan_pass(1, [1, 0] + list(range(nt1 - 1, 1, -1)))
        self.S.barrier()
        self.scan_pass(0, list(range(nt1)))

    def l1_prep(self):
        S = self.S
        li = 1
        W_ = 259
        nt1 = self.nt1 or NT
        blocks = [(0, 1)] + [(2 + 2 * j, 0) for j in range((nt1 - 2) // 2)]
        with ExitStack() as st:
            sb = lambda n, s, d=F32: self.sb(st, n, s, d)
            ps = lambda n, s, d=F32: self.ps(st, n, s, d)
            w_in = sb("w_in1", [128, 8, 5184], BF16)
            self.load_w_bf16(w_in, self.w_in1, 8, 5184)
            cw = sb("scw", [128, 24, 4]); cb = sb("scb", [128, 24]); dtb = sb("dtb", [128, 64])
            S.dma("sp", cw[:], self.scw, writes=[cw])
            S.dma("sp", cb[:], self.scb, writes=[cb])
            S.dma("sp", dtb[:], self.dtb, writes=[dtb])
            hTb = [sb(f"hTb{j}", [128, 8, W_], BF16) for j in range(2)]
            xt = [sb(f"xt{j}", [128, D]) for j in range(2)]
            sq = sb("sq", [128, D], BF16); ss = sb("ss", [128, 1]); xn = sb("xn", [128, D], BF16)
            cva = [sb(f"cva{j}", [128, 256]) for j in range(2)]
            cvb = [sb(f"cvb{j}", [128, 256]) for j in range(2)]
            xbcT = sb("xbcT", [128, 24, 256], BF16)
            rec = [sb(f"rec{j}", [128, 4608], BF16) for j in range(2)]
            dtt = [sb(f"dtt{j}", [128, 64]) for j in range(2)]
            pT = ps("pT", [128, 8, 128], BF16)
            pU = [ps(f"pU{j}", [128, 512]) for j in range(2)]
            pTt = ps("pTt", [128, 8, 128], BF16)
            pZ = ps("pZ", [128, 512])
            pD = ps("pD", [128, 512])
            nb = len(blocks)

            def stageA(j):
                t0, s = blocks[j]
                hb = hTb[j % 2]
                A = self.A1[li][s]
                md = self.mod[li]
                for tt_ in range(2):
                    ti = t0 + tt_
                    x_t = xt[tt_]
                    S.dma("sp", x_t[:], self.xb[ti * 128:(ti + 1) * 128, :], reads=[self.b_xb[ti]], writes=[x_t])
                    self.rstd(x_t[:], D, sq[:], ss, [x_t], [sq])
                    self.ts("dve", xn[:], x_t[:], ss[:, 0:1], None, ALU.mult, None, [x_t, ss], [xn])
                    self.trn([(pT[:, k, :], xn[:, k * 128:(k + 1) * 128], self.idb[:]) for k in range(8)],
                             [xn, self.idb], [pT])
                    for k in range(8):
                        self.act(hb[:, k, 2 + tt_ * 128:2 + (tt_ + 1) * 128], pT[:, k, :], AF.Identity,
                                 [pT, A, md], [hb], scale=A[:, k:k + 1], bias=md[:, k, s:s + 1])

            def halos(j):
                t0, s = blocks[j]
                hb = hTb[j % 2]
                cont = j > 0 and blocks[j - 1][1] == s and blocks[j - 1][0] + 2 == t0
                if cont:
                    hp = hTb[(j - 1) % 2]
                    self.cp("pool", hb[:, :, 0:2], hp[:, :, 256:258], [hp], [hb])
                    self.cp("pool", hp[:, :, 258:259], hb[:, :, 2:3], [hb], [hp])
                else:
                    self.memset("pool", hb[:, :, 0:2], 0.0, [hb])
                    if j > 0:
                        hp = hTb[(j - 1) % 2]
                        self.memset("pool", hp[:, :, 258:259], 0.0, [hp])

            def stageB(j):
                t0, s = blocks[j]
                hb = hTb[j % 2]
                for c in range(24):
                    pu_ = pU[c % 2]
                    ta = cva[c % 2]
                    tb = cvb[c % 2]
                    self.mm([(pu_[:, 0:W_], w_in[:, k, 2048 + c * 128:2048 + (c + 1) * 128], hb[:, k, :],
                              k == 0, k == 7) for k in range(8)], [w_in, hb], [pu_])
                    self.act(ta[:], pu_[:, 2:258], AF.Identity, [pu_, cw, cb], [ta],
                             scale=cw[:, c, 2:3], bias=cb[:, c:c + 1])
                    self.stt(tb[:], pu_[:, 0:256], cw[:, c, 0:1], ta[:], ALU.mult, ALU.add, [pu_, cw, ta], [tb])
                    self.stt(ta[:], pu_[:, 1:257], cw[:, c, 1:2], tb[:], ALU.mult, ALU.add, [pu_, cw, tb], [ta])
                    self.stt(tb[:], pu_[:, 3:259], cw[:, c, 3:4], ta[:], ALU.mult, ALU.add, [pu_, cw, ta], [tb])
                    self.act(xbcT[:, c, :], tb[:], AF.Silu, [tb], [xbcT])
                for tt_ in range(2):
                    ti = t0 + tt_
                    rt = rec[tt_]
                    for g0 in (0, 8, 16):
                        n = 8 if g0 < 16 else 4
                        self.trn([(pTt[:, j2, :], xbcT[:, g0 + j2, tt_ * 128:(tt_ + 1) * 128], self.idb[:])
                                  for j2 in range(n)], [xbcT, self.idb], [pTt])
                        self.cp("act", rt[:, g0 * 128:(g0 + n) * 128].rearrange("p (a b) -> p a b", b=128),
                                pTt[:, 0:n, :], [pTt], [rt])
                    for n in range(4):
                        self.mm([(pZ[:], hb[:, k, 2 + tt_ * 128:2 + (tt_ + 1) * 128],
                                  w_in[:, k, n * 512:(n + 1) * 512], k == 0, k == 7) for k in range(8)],
                                [hb, w_in], [pZ])
                        self.act(rt[:, 2560 + n * 512:2560 + (n + 1) * 512], pZ[:], AF.Silu, [pZ], [rt])
                    self.mm([(pD[:, 0:64], hb[:, k, 2 + tt_ * 128:2 + (tt_ + 1) * 128], w_in[:, k, 5120:5184],
                              k == 0, k == 7) for k in range(8)], [hb, w_in], [pD])
                    d_t = dtt[tt_]
                    self.tt("dve", d_t[:], pD[:, 0:64], dtb[:], ALU.add, [pD, dtb], [d_t])
                    self.act(d_t[:], d_t[:], AF.Exp, [d_t], [d_t])
                    self.act(d_t[:], d_t[:], AF.Ln, [d_t], [d_t], bias=1.0)
                    S.dma("sp", self.dt_d[ti], d_t[:], reads=[d_t], writes=[self.b_rec[ti]])
                    S.dma("sp", self.rec_tok[ti], rt[:], reads=[rt], writes=[self.b_rec[ti]])
                    S.dma("sp", self.rec_fm[ti], xbcT[:, 16:24, tt_ * 128:(tt_ + 1) * 128], reads=[xbcT],
                          writes=[self.b_rec[ti]])

            stageA(0)
            halos(0)
            for j in range(nb):
                S.mark(4000)
                if j + 1 < nb:
                    stageA(j + 1)
                    halos(j + 1)
                else:
                    self.memset("pool", hTb[j % 2][:, :, 258:259], 0.0, [hTb[j % 2]])
                stageB(j)

    def scan_pass(self, d, order):
        S = self.S
        li = 1
        final = d == 0
        with ExitStack() as st:
            sb = lambda n, s, dt_=F32: self.sb(st, n, s, dt_)
            ps = lambda n, s, dt_=F32: self.ps(st, n, s, dt_)
            S_f = sb("S_f", [128, 2048]); S_b = sb("S_b", [128, 2048], BF16)
            self.memset("dve", S_f[:], 0.0, [S_f])
            self.memset("dve", S_b[:], 0.0, [S_b])
            msk = sb("msk", [128, 4, 128])
            S.dma("sp", msk[:], self.msk, writes=[msk])
            Md = msk[:, d, :]
            Neg = msk[:, 2 + d, :]
            Abc = sb("Abc", [128, 64])
            S.dma("sp", Abc[:], self.alog, writes=[Abc])
            self.act(Abc[:], Abc[:], AF.Exp, [Abc], [Abc])
            self.ts("dve", Abc[:], Abc[:], -1.0, None, ALU.mult, None, [Abc], [Abc])
            rec = [sb(f"rec{j}", [128, 4608], BF16) for j in range(2)]
            recf = [sb(f"recf{j}", [128, 8, 128], BF16) for j in range(2)]
            dtt = [sb(f"dtt{j}", [128, 64]) for j in range(2)]
            a_t = sb("a_t", [128, 32]); acum = sb("acum", [128, 32]); nacum = sb("nacum", [128, 32])
            tot = sb("tot", [128, 32]); E_t = sb("E_t", [128, 32]); dte = sb("dte", [128, 32])
            etot = sb("etot", [128, 32])
            xdt = sb("xdt", [128, 2048], BF16); xdt2 = sb("xdt2", [128, 2048], BF16)
            rhs4 = [sb(f"rhs4{j}", [128, 4, 128]) for j in range(2)]
            Dp = [sb(f"Dp{j}", [128, 4, 128]) for j in range(2)]
            Lt = [sb(f"Lt{j}", [128, 4, 128]) for j in range(2)]
            Wt = [sb(f"Wt{j}", [128, 4, 128], BF16) for j in range(2)]
            cb_sb = sb("cb_sb", [128, 4, 128])
            tmpg = sb("tmpg", [128, 512])
            yacc = [sb(f"yacc{j}", [128, 2048]) for j in range(2)]
            pAc = ps("pAc", [128, 512])
            pCB = ps("pCB", [128, 4, 128])
            pBc = [ps(f"pBc{j}", [128, 4, 128]) for j in range(2)]
            pYI = [ps(f"pYI{j}", [128, 512]) for j in range(2)]
            pYS = ps("pYS", [128, 512])
            pSt = ps("pSt", [128, 512])
            if final:
                Dsk = sb("Dsk", [128, 32]); snw = sb("snw", [128, 16])
                S.dma("sp", Dsk[:], self.dsk, writes=[Dsk])
                S.dma("sp", snw[:], self.snw, writes=[snw])
                wout = sb("wout1", [128, 16, D], BF16)
                self.load_w_bf16(wout, self.w_out1, 16, D)
                Gbc = sb("G1bc", [128, D])
                pYv = None
                yb = sb("yb", [128, 2048]); yg = sb("yg", [128, 2048]); xsD = sb("xsD", [128, 2048])
                yT = sb("yT", [128, 16, 128], BF16)
                xr = sb("xr", [128, D]); tmp = sb("tmp", [128, D]); xo = [sb(f"xo{j}", [128, D]) for j in range(2)]
                sq = sb("sq", [128, 2048], BF16); ss1 = sb("ss1", [128, 1]); ssa = sb("ssa", [128, 1])
                ssb = sb("ssb", [128, 1]); sc = sb("sc", [128, 1])
                tmpb = [sb(f"bct{j}", [128, 128]) for j in range(2)]
                src = self.G1[li][0]
                for k in range(8):
                    t = tmpb[k % 2]
                    pb = pBc[k // 4]
                    self.ts("dve", t[:], self.ones[:], src[:, k:k + 1], None, ALU.mult, None, [self.ones, src], [t])
                    self.mm([(pb[:, k % 4, :], t[:], self.idf[:], True, True)], [t, self.idf], [pb])
                for n in range(2):
                    self.cp("act", Gbc[:, n * 512:(n + 1) * 512].rearrange("p (a b) -> p a b", b=128), pBc[n][:],
                            [pBc[n]], [Gbc])

            for kk, i in enumerate(order):
                S.mark(4000)
                need_y = i >= 2
                rt = rec[kk % 2]; rf = recf[kk % 2]; d_t = dtt[kk % 2]
                S.dma("sp", rt[:], self.rec_tok[i], reads=[self.b_rec[i]], writes=[rt])
                S.dma("sp", rf[:], self.rec_fm[i], reads=[self.b_rec[i]], writes=[rf])
                S.dma("sp", d_t[:], self.dt_d[i], reads=[self.b_rec[i]], writes=[d_t])
                dsl = d_t[:, d * 32:(d + 1) * 32]
                self.tt("dve", a_t[:], dsl, Abc[:, d * 32:(d + 1) * 32], ALU.mult, [d_t, Abc], [a_t])
                self.mm([(pAc[:, 0:32], Md, a_t[:], True, True), (pAc[:, 32:64], self.ones[:], a_t[:], True, True)],
                        [msk, a_t, self.ones], [pAc])
                self.cp("dve", acum[:], pAc[:, 0:32], [pAc], [acum])
                self.cp("dve", tot[:], pAc[:, 32:64], [pAc], [tot])
                self.ts("dve", nacum[:], acum[:], -1.0, None, ALU.mult, None, [acum], [nacum])
                self.tt("dve", xdt[:].rearrange("p (h q) -> p h q", q=64),
                        rt[:, 0:2048].rearrange("p (h q) -> p h q", q=64),
                        dsl.unsqueeze(2).to_broadcast([128, 32, 64]), ALU.mult, [rt, d_t], [xdt])
                ya = yacc[kk % 2]
                if need_y:
                    self.act(E_t[:], acum[:], AF.Exp, [acum], [E_t])
                    self.mm([(pCB[:, g, :], rf[:, g, :], rf[:, 4 + g, :], True, True) for g in range(4)], [rf], [pCB])
                    self.cp("act", cb_sb[:], pCB[:], [pCB], [cb_sb])
                    for hb4 in range(8):
                        h0 = hb4 * 4
                        g = hb4 // 2
                        r4 = rhs4[hb4 % 2]; pb = pBc[hb4 % 2]; dd = Dp[hb4 % 2]; lt = Lt[hb4 % 2]; w = Wt[hb4 % 2]
                        self.tt("dve", r4[:], Md.unsqueeze(1).to_broadcast([128, 4, 128]),
                                a_t[:, h0:h0 + 4].unsqueeze(2).to_broadcast([128, 4, 128]), ALU.mult,
                                [msk, a_t], [r4])
                        self.mm([(pb[:], self.ones[:], r4[:], True, True)], [self.ones, r4], [pb])
                        self.tt("dve", dd[:], pb[:], Neg.unsqueeze(1).to_broadcast([128, 4, 128]), ALU.add,
                                [pb, msk], [dd])
                        for hh in range(4):
                            self.act(lt[:, hh, :], dd[:, hh, :], AF.Exp, [dd, nacum], [lt],
                                     bias=nacum[:, h0 + hh:h0 + hh + 1])
                        self.tt("dve", w[:], lt[:], cb_sb[:, g, :].unsqueeze(1).to_broadcast([128, 4, 128]),
                                ALU.mult, [lt, cb_sb], [w])
                        pyi = pYI[g % 2]
                        self.mm([(pyi[:, ((h0 + hh) % 8) * 64:((h0 + hh) % 8 + 1) * 64], w[:, hh, :],
                                  xdt[:, (h0 + hh) * 64:(h0 + hh + 1) * 64], True, True) for hh in range(4)],
                                [w, xdt], [pyi])
                        if hb4 % 2 == 1:
                            self.mm([(pYS[:], rf[:, 4 + g, :], S_b[:, g * 512:(g + 1) * 512], True, True)],
                                    [rf, S_b], [pYS])
                            self.tt("dve", tmpg[:].rearrange("p (h q) -> p h q", q=64),
                                    pYS[:].rearrange("p (h q) -> p h q", q=64),
                                    E_t[:, g * 8:(g + 1) * 8].unsqueeze(2).to_broadcast([128, 8, 64]), ALU.mult,
                                    [pYS, E_t], [tmpg])
                            self.tt("dve", ya[:, g * 512:(g + 1) * 512], tmpg[:], pyi[:], ALU.add, [tmpg, pyi], [ya])
                self.tt("dve", dte[:], tot[:], acum[:], ALU.subtract, [tot, acum], [dte])
                self.act(dte[:], dte[:], AF.Exp, [dte], [dte])
                self.act(etot[:], tot[:], AF.Exp, [tot], [etot])
                self.tt("dve", xdt2[:].rearrange("p (h q) -> p h q", q=64),
                        xdt[:].rearrange("p (h q) -> p h q", q=64),
                        dte[:].unsqueeze(2).to_broadcast([128, 32, 64]), ALU.mult, [xdt, dte], [xdt2])
                for g in range(4):
                    self.mm([(pSt[:], rt[:, 2048 + g * 128:2048 + (g + 1) * 128], xdt2[:, g * 512:(g + 1) * 512],
                              True, True)], [rt, xdt2], [pSt])
                    sg_ = S_f[:, g * 512:(g + 1) * 512]
                    self.tt("dve", sg_.rearrange("p (h q) -> p h q", q=64), sg_.rearrange("p (h q) -> p h q", q=64),
                            etot[:, g * 8:(g + 1) * 8].unsqueeze(2).to_broadcast([128, 8, 64]), ALU.mult,
                            [S_f, etot], [S_f])
                    self.tt("dve", sg_, sg_, pSt[:], ALU.add, [S_f, pSt], [S_f])
                    self.cp("act", S_b[:, g * 512:(g + 1) * 512], sg_, [S_f], [S_b])
                if not need_y:
                    continue
                if not final:
                    S.dma("sp", self.yb_d[i], ya[:], reads=[ya], writes=[self.b_yb[i]])
                    continue
                S.dma("sp", yb[:], self.yb_d[i], reads=[self.b_yb[i]], writes=[yb])
                S.dma("sp", xr[:], self.xb[i * 128:(i + 1) * 128, :], reads=[self.b_xb[i]], writes=[xr])
                self.tt("dve", yg[:], ya[:], yb[:], ALU.add, [ya, yb], [yg])
                self.tt("dve", xsD[:].rearrange("p (h q) -> p h q", q=64),
                        rt[:, 0:2048].rearrange("p (h q) -> p h q", q=64),
                        Dsk[:].unsqueeze(2).to_broadcast([128, 32, 64]), ALU.mult, [rt, Dsk], [xsD])
                self.tt("dve", yg[:], yg[:], xsD[:], ALU.add, [yg, xsD], [yg])
                self.tt("dve", yg[:], yg[:], rt[:, 2560:4608], ALU.mult, [yg, rt], [yg])
                self.rstd(yg[:], 2048, sq[:], ss1, [yg], [sq])
                for c4 in range(4):
                    pt = pYI[c4 % 2]
                    self.trn([(pt[:, jj * 128:(jj + 1) * 128], yg[:, (c4 * 4 + jj) * 128:(c4 * 4 + jj + 1) * 128],
                               self.idf[:]) for jj in range(4)], [yg, self.idf], [pt])
                    for jj in range(4):
                        c = c4 * 4 + jj
                        self.act(yT[:, c, :], pt[:, jj * 128:(jj + 1) * 128], AF.Identity, [pt, snw], [yT],
                                 scale=snw[:, c:c + 1])
                for n in range(2):
                    self.mm([(pBc[n][:].rearrange("p a b -> p (a b)"), yT[:, k, :], wout[:, k, n * 512:(n + 1) * 512],
                              k == 0, k == 15) for k in range(16)], [yT, wout], [pBc[n]])
                self.act(sq[:, 0:512], pBc[0][:].rearrange("p a b -> p (a b)"), AF.Square, [pBc[0]], [sq, ssa],
                         accum_out=ssa[:, 0:1])
                self.act(sq[:, 512:1024], pBc[1][:].rearrange("p a b -> p (a b)"), AF.Square, [pBc[1]], [sq, ssb],
                         accum_out=ssb[:, 0:1])
                self.tt("dve", ssa[:], ssa[:], ssb[:], ALU.add, [ssa, ssb], [ssa])
                self.stt(ssa[:], ssa[:], 1.0 / D, ss1[:], ALU.mult, ALU.mult, [ssa, ss1], [ssa])
                self.stt(ssa[:], ssa[:], 1.0, ss1[:], ALU.mult, ALU.mult, [ssa, ss1], [ssa])
                self.ts("dve", ssa[:], ssa[:], EPS, None, ALU.add, None, [ssa], [ssa])
                self.act(ssa[:], ssa[:], AF.Sqrt, [ssa], [ssa])
                self.S.op("dve", lambda e: e.reciprocal(out=ssa[:], in_=ssa[:]), [ssa], [ssa])
                self.tt("dve", sc[:], ssa[:], ss1[:], ALU.mult, [ssa, ss1], [sc])
                for n in range(2):
                    self.stt(tmp[:, n * 512:(n + 1) * 512], pBc[n][:].rearrange("p a b -> p (a b)"), sc[:, 0:1],
                             Gbc[:, n * 512:(n + 1) * 512], ALU.mult, ALU.mult, [pBc[n], sc, Gbc], [tmp])
                x_o = xo[kk % 2]
                self.tt("pool", x_o[:], tmp[:], xr[:], ALU.add, [tmp, xr], [x_o])
                S.dma("sp", self.xm1[i * 128:(i + 1) * 128, :], x_o[:], reads=[x_o], writes=[self.b_xm1[i]])


def prep_inputs(inputs, b):
    f32 = np.float32
    g = lambda k: np.asarray(inputs[k], f32)
    m = {}
    m["xa"] = np.ascontiguousarray(np.concatenate([g("ctx")[b], g("x")[b]], axis=0))
    cond = np.stack([g("c")[b], g("c_ctx")], axis=-1)
    m["cond"] = np.ascontiguousarray(cond.reshape(8, 128, 2).transpose(1, 0, 2))
    m["ada_w"] = g("ada_w")
    m["ada_b"] = np.ascontiguousarray(g("ada_b").reshape(2, 48, 128).transpose(0, 2, 1))
    m["nw"] = np.ascontiguousarray(g("norm_w").reshape(2, 4, 8, 128).transpose(3, 0, 1, 2))
    m["ident"] = np.eye(128, dtype=f32)
    m["w_in0"] = g("attn_w_in")[0]
    m["w_out0"] = g("attn_w_out")[0]
    pw = g("pool_w")[0]
    pwblk = np.zeros((128, 2, 128), f32)
    for gg in range(4):
        c, h = gg // 2, gg % 2
        pwblk[h * 64:(h + 1) * 64, c, h * 64:(h + 1) * 64] = pw[gg]
    m["pwblk"] = pwblk
    m["pscale"] = _fm(g("pool_scale")[0], 2)
    m["qkg"] = np.ascontiguousarray(np.broadcast_to(
        np.stack([g("q_gain")[0], g("k_gain")[0]], 0)[None], (128, 2, 128)))
    m["bands"] = np.ascontiguousarray(_pool_bands().reshape(20, 128, 128).transpose(1, 0, 2))
    cos, sin = _rope_tables()
    m["cos"] = cos
    m["sin"] = sin
    m["w_up"] = g("ffn_w_up")
    m["w_dn"] = g("ffn_w_down")
    m["fcw"] = np.ascontiguousarray(g("ffn_conv_w").reshape(2, 2 * NFF, 128, 3).transpose(0, 2, 1, 3))
    m["fcb"] = np.ascontiguousarray(g("ffn_conv_b").reshape(2, 2 * NFF, 128).transpose(0, 2, 1))
    m["w_in1"] = g("ssm_w_in")[0]
    m["w_out1"] = g("ssm_w_out")[0]
    m["scw"] = np.ascontiguousarray(g("ssm_conv_w")[0].reshape(24, 128, 4).transpose(1, 0, 2))
    m["scb"] = _fm(g("ssm_conv_b")[0], 24)
    bc = lambda v: np.ascontiguousarray(np.broadcast_to(np.asarray(v, f32).reshape(1, -1), (128, v.size)))
    m["dtb"] = bc(g("ssm_dt_bias")[0])
    m["alog"] = bc(g("ssm_A_log")[0])
    m["dsk"] = bc(g("ssm_D")[0])
    m["snw"] = _fm(g("ssm_norm_w")[0], 16)
    ii = np.arange(128)
    mf = (ii[:, None] <= ii[None, :]).astype(f32)
    mb = (ii[:, None] >= ii[None, :]).astype(f32)
    m["msk"] = np.ascontiguousarray(np.stack([mf, mb, (mf - 1) * 30000.0, (mb - 1) * 30000.0], axis=1))
    return m


_NC_CACHE = {}


def kernel(**inputs):
    key = "full"
    if key not in _NC_CACHE:
        _NC_CACHE[key] = MK().build()
    nc = _NC_CACHE[key]
    in_maps = [prep_inputs(inputs, b) for b in range(8)]
    res = run_bass_kernel_spmd(nc, in_maps, core_ids=list(range(8)))
    return np.stack([np.asarray(r["out"], np.float32) for r in res.results], axis=0)
```

```python
import numpy as np
from contextlib import ExitStack
import concourse.bass as bass
import concourse.mybir as mybir
from concourse.bass_utils import run_bass_kernel_spmd

F32 = mybir.dt.float32
BF16 = mybir.dt.bfloat16
AF = mybir.ActivationFunctionType
ALU = mybir.AluOpType
AX = mybir.AxisListType

ENGS = ("pe", "act", "dve", "pool", "sp")
NSLOT = 8
import os
USE_POOL = os.environ.get("USE_POOL", "0") == "1"
STRICT = os.environ.get("MK_STRICT", "1") == "1"
STQ = os.environ.get("MK_STQ", "pool")

D = 1024
T = 8192
L = 256
NTOK = T + L
NT = NTOK // 128
EPS = 1e-6
DFF = 2816
NFF = DFF // 128
POOL_WINDOWS = (2, 4, 8, 16)


class Buf:
    __slots__ = ("name", "w", "r")

    def __init__(self, name=""):
        self.name = name
        self.w = None
        self.r = []


class TL:
    def __init__(self, t, name="", excl=False):
        self.t = t
        self.b = Buf(name)
        self.excl = excl

    def __getitem__(self, idx):
        return self.t[idx]


def _b(x):
    return x.b if isinstance(x, TL) else x


class Sched:
    LIMIT = 16000
    DLIMIT = 1000

    def __init__(self, nc, stack, strict=True):
        self.nc = nc
        self.stack = stack
        self.strict = strict
        self.ops = {e: [] for e in ENGS}
        self.cnt = {e: 0 for e in ENGS}
        self.tot = {e: 0 for e in ENGS}
        self.sems = {e: [stack.enter_context(nc.semaphore("s_" + e + "0"))] for e in ENGS}
        self.dq = ("sp", "act", "pool")
        self.dsems = {q: [[stack.enter_context(nc.semaphore(f"d_{q}_0_{i}")) for i in range(NSLOT)]]
                      for q in self.dq}
        self.dcnt = {q: 0 for q in self.dq}
        self.seen = {e: {} for e in ENGS}
        self.ninst = 0
        self.marks = []
        self.last_mark_n = 0

    def _ev(self, ev):
        if ev[0] == "e":
            _, eng, gen, val = ev
            return ("e", eng, gen), val, self.sems[eng][gen]
        _, q, gen, slot, val = ev
        return ("d", q, gen, slot), val, self.dsems[q][gen][slot]

    def _deps(self, eng, reads, writes):
        evs = []
        for b in reads:
            b = _b(b)
            if b.w is not None:
                evs.append(b.w)
        for b in writes:
            b = _b(b)
            if b.w is not None:
                evs.append(b.w)
            evs.extend(b.r)
        need = {}
        for ev in evs:
            if ev[0] == "e" and ev[1] == eng:
                if eng in ("pe", "sp") or not self.strict:
                    continue
            key, val, sem = self._ev(ev)
            if self.seen[eng].get(key, 0) >= val:
                continue
            if key not in need or need[key][0] < val:
                need[key] = (val, sem)
        waits = []
        for key, (val, sem) in need.items():
            self.seen[eng][key] = val
            waits.append((sem, val))
        return waits

    def _commit(self, ev, reads, writes):
        for b in reads:
            _b(b).r.append(ev)
        for b in writes:
            b = _b(b)
            b.w = ev
            b.r = []

    def op(self, eng, fn, reads=(), writes=()):
        if eng == "pool" and not USE_POOL:
            eng = "dve"
        ex = [b for b in reads if getattr(b, "excl", False)]
        if ex:
            reads = [b for b in reads if not getattr(b, "excl", False)]
            writes = list(writes) + ex
        waits = self._deps(eng, reads, writes)
        if self.cnt[eng] >= self.LIMIT:
            g = len(self.sems[eng])
            self.sems[eng].append(self.stack.enter_context(self.nc.semaphore(f"s_{eng}{g}")))
            self.cnt[eng] = 0
        self.cnt[eng] += 1
        self.tot[eng] += 1
        idx = self.cnt[eng]
        gen = len(self.sems[eng]) - 1
        sem = self.sems[eng][gen]

        def closure(e, waits=waits, fn=fn, sem=sem):
            for s, v in waits:
                e.wait_ge(s, v)
            fn(e).then_inc(sem, 1)

        self.ops[eng].append(closure)
        self._commit(("e", eng, gen, idx), reads, writes)
        self.ninst += 1

    def dma(self, q, out, in_, reads=(), writes=(), **kw):
        waits = self._deps(q, reads, writes)
        m = self.dcnt[q]
        self.dcnt[q] += 1
        per_gen = NSLOT * self.DLIMIT
        gen = m // per_gen
        mm_ = m % per_gen
        if gen >= len(self.dsems[q]):
            self.dsems[q].append([self.stack.enter_context(self.nc.semaphore(f"d_{q}_{gen}_{i}"))
                                  for i in range(NSLOT)])
        slot = mm_ % NSLOT
        sem = self.dsems[q][gen][slot]
        prev = 16 * (mm_ // NSLOT)
        key = ("d", q, gen, slot)
        if prev > 0 and self.seen[q].get(key, 0) < prev:
            waits.append((sem, prev))
            self.seen[q][key] = prev
        if prev == 0 and gen > 0:
            pass

        def closure(e, waits=waits, sem=sem, out=out, in_=in_, kw=kw):
            for s, v in waits:
                e.wait_ge(s, v)
            e.dma_start(out=out, in_=in_, **kw).then_inc(sem, 16)

        self.ops[q].append(closure)
        self._commit(("d", q, gen, slot, prev + 16), reads, writes)
        self.ninst += 1

    def _all_targets(self):
        tg = []
        for e in ENGS:
            for gen, sem in enumerate(self.sems[e]):
                val = self.LIMIT if gen < len(self.sems[e]) - 1 else self.cnt[e]
                if val > 0:
                    tg.append((("e", e, gen), val, sem))
        per_gen = NSLOT * self.DLIMIT
        for q in self.dq:
            for gen, sl in enumerate(self.dsems[q]):
                n_in_gen = min(per_gen, self.dcnt[q] - gen * per_gen)
                for slot in range(NSLOT):
                    n = (n_in_gen - slot + NSLOT - 1) // NSLOT
                    if n > 0:
                        tg.append((("d", q, gen, slot), 16 * n, sl[slot]))
        return tg

    def barrier(self, engines=ENGS):
        tg = self._all_targets()
        for eng in engines:
            waits = []
            for key, val, sem in tg:
                if key[0] == "e" and key[1] == eng and eng in ("pe", "sp"):
                    continue
                if self.seen[eng].get(key, 0) >= val:
                    continue
                self.seen[eng][key] = val
                waits.append((sem, val))
            if not waits:
                continue

            def closure(e, waits=waits):
                for s, v in waits:
                    e.wait_ge(s, v)

            self.ops[eng].append(closure)

    def finish(self):
        self.barrier(engines=("sp",))

    def mark(self, min_ops=0):
        if self.ninst - self.last_mark_n < min_ops:
            return
        self.last_mark_n = self.ninst
        self.marks.append({e: len(self.ops[e]) for e in ENGS})

    def emit(self):
        nc = self.nc
        bounds = self.marks + [{e: len(self.ops[e]) for e in ENGS}]
        prev = {e: 0 for e in ENGS}
        for bd in bounds:
            seg = {e: self.ops[e][prev[e]:bd[e]] for e in ENGS}
            prev = bd
            if not any(seg[e] for e in ENGS):
                continue
            with nc.Block() as block:
                @block.tensor
                def _(e, seg=seg):
                    for c in seg["pe"]:
                        c(e)

                @block.scalar
                def _(e, seg=seg):
                    for c in seg["act"]:
                        c(e)

                @block.vector
                def _(e, seg=seg):
                    for c in seg["dve"]:
                        c(e)

                @block.gpsimd
                def _(e, seg=seg):
                    for c in seg["pool"]:
                        c(e)

                @block.sync
                def _(e, seg=seg):
                    for c in seg["sp"]:
                        c(e)


def _fm(v, nchunk):
    return np.ascontiguousarray(np.asarray(v, np.float32).reshape(nchunk, 128).T)


def _pool_bands():
    out = np.zeros((4, 5, 128, 128), np.float32)
    n = 384
    for g, w in enumerate(POOL_WINDOWS):
        left = w // 2
        right = w - 1 - left
        M = np.zeros((n, n), np.float64)
        for t in range(n):
            lo = max(t - left, 0)
            hi = min(t + right, n - 1)
            M[lo:hi + 1, t] = 1.0 / (hi - lo + 1)
            M[t, t] -= 1.0
        out[g, 0] = M[0:128, 128:256]
        out[g, 1] = M[128:256, 128:256]
        out[g, 2] = M[256:384, 128:256]
        out[g, 3] = M[0:128, 0:128]
        out[g, 4] = M[256:384, 256:384]
    return out


def _rope_tables():
    rows = T // 64
    row = np.repeat(np.arange(rows, dtype=np.float32), 64)
    col = np.tile(np.arange(64, dtype=np.float32), rows)
    half = 64
    inv = (np.float32(10000.0) ** (-np.arange(0, half, 2, dtype=np.float32) / np.float32(half))).astype(np.float32)
    ang = np.concatenate([row[:, None] * inv, col[:, None] * inv], axis=-1).astype(np.float32)
    return np.cos(ang).astype(np.float32), np.sin(ang).astype(np.float32)


class MK:
    def __init__(self, phases=("p0", "l0", "f0", "l1", "f1"), dbg=(), nblk=None, nt1=None):
        self.phases = phases
        self.dbg = dbg
        self.nblk = nblk
        self.nt1 = nt1
        self.nc = bass.Bass("TRN2", target_bir_lowering=False)
        self.din = {}
        self.dout = {}

    def inp(self, name, shape, dt=F32):
        ap = self.nc.dram_tensor(name, list(shape), dt, kind="ExternalInput").ap()
        self.din[name] = ap
        return ap

    def scratch(self, name, shape, dt=F32):
        kind = "ExternalOutput" if name in self.dbg else "Internal"
        ap = self.nc.dram_tensor(name, list(shape), dt, kind=kind).ap()
        return ap

    def _uname(self, name):
        self._uid = getattr(self, "_uid", 0) + 1
        return f"{name}_u{self._uid}"

    def sb(self, st, name, shape, dt=F32):
        return TL(st.enter_context(self.nc.sbuf_tensor(self._uname("s_" + name), list(shape), dt)), name)

    def ps(self, st, name, shape, dt=F32):
        nbytes = int(np.prod(shape[1:])) * (2 if dt == BF16 else 4)
        assert nbytes % 2048 == 0, (name, shape)
        return TL(st.enter_context(self.nc.psum_tensor(self._uname("p_" + name), list(shape), dt)), name, excl=True)

    def act(self, out, in_, func, r, w, **kw):
        self.S.op("act", lambda e: e.activation(out=out, in_=in_, func=func, **kw), r, w)

    def ts(self, eng, out, in0, s1, s2, op0, op1, r, w):
        if op1 is None:
            self.S.op(eng, lambda e: e.tensor_scalar(out=out, in0=in0, scalar1=s1, scalar2=None, op0=op0), r, w)
        else:
            self.S.op(eng, lambda e: e.tensor_scalar(out=out, in0=in0, scalar1=s1, scalar2=s2, op0=op0, op1=op1), r, w)

    def tt(self, eng, out, in0, in1, op, r, w):
        self.S.op(eng, lambda e: e.tensor_tensor(out=out, in0=in0, in1=in1, op=op), r, w)

    def stt(self, out, in0, scalar, in1, op0, op1, r, w):
        self.S.op("dve", lambda e: e.scalar_tensor_tensor(out=out, in0=in0, scalar=scalar, in1=in1,
                                                          op0=op0, op1=op1), r, w)

    def cp(self, eng, out, in_, r, w):
        if eng == "act":
            self.S.op("act", lambda e: e.copy(out=out, in_=in_), r, w)
        else:
            self.S.op(eng, lambda e: e.tensor_copy(out=out, in_=in_), r, w)

    def memset(self, eng, ap, val, w):
        self.S.op(eng, lambda e: e.memset(ap, val), (), w)

    def mm(self, items, r, w):
        def fn(e, items=items):
            ins = None
            for (o, l, rr, s0, s1) in items:
                ins = e.matmul(o, lhsT=l, rhs=rr, start=s0, stop=s1)
            return ins
        self.S.op("pe", fn, r, w)

    def trn(self, items, r, w):
        def fn(e, items=items):
            ins = None
            for (o, i, idn) in items:
                ins = e.transpose(out=o, in_=i, identity=idn)
            return ins
        self.S.op("pe", fn, r, w)

    def rstd(self, src, n, junk, ss, r, w_extra=()):
        self.act(junk, src, AF.Square, r, [ss] + list(w_extra), accum_out=ss[:, 0:1])
        self.ts("dve", ss[:, 0:1], ss[:, 0:1], 1.0 / n, EPS, ALU.mult, ALU.add, [ss], [ss])
        self.act(ss[:, 0:1], ss[:, 0:1], AF.Sqrt, [ss], [ss])
        self.S.op("dve", lambda e: e.reciprocal(out=ss[:, 0:1], in_=ss[:, 0:1]), [ss], [ss])

    def build(self):
        nc = self.nc
        inp = self.inp
        self.xa = inp("xa", [NTOK, D])
        self.cond = inp("cond", [128, 8, 2])
        self.ada_w = inp("ada_w", [2, D, 6 * D])
        self.ada_b = inp("ada_b", [2, 128, 48])
        self.nw = inp("nw", [128, 2, 4, 8])
        self.ident = inp("ident", [128, 128])
        self.w_in0 = inp("w_in0", [D, 1536])
        self.w_out0 = inp("w_out0", [D, D])
        self.pwblk = inp("pwblk", [128, 2, 128])
        self.pscale = inp("pscale", [128, 2])
        self.qkg = inp("qkg", [128, 2, 128])
        self.bands = inp("bands", [128, 20, 128])
        self.cos = inp("cos", [T, 64])
        self.sin = inp("sin", [T, 64])
        self.w_up = inp("w_up", [2, D, 2 * DFF])
        self.w_dn = inp("w_dn", [2, DFF, D])
        self.fcw = inp("fcw", [2, 128, 2 * NFF, 3])
        self.fcb = inp("fcb", [2, 128, 2 * NFF])

        self.w_in1 = inp("w_in1", [D, 5184])
        self.w_out1 = inp("w_out1", [2048, D])
        self.scw = inp("scw", [128, 24, 4])
        self.scb = inp("scb", [128, 24])
        self.dtb = inp("dtb", [128, 64])
        self.alog = inp("alog", [128, 64])
        self.dsk = inp("dsk", [128, 32])
        self.snw = inp("snw", [128, 16])
        self.msk = inp("msk", [128, 4, 128])
        self.rec_tok = self.scratch("rec_tok", [NT, 128, 4608], BF16)
        self.rec_fm = self.scratch("rec_fm", [NT, 128, 8, 128], BF16)
        self.dt_d = self.scratch("dt_d", [NT, 128, 64])
        self.yb_d = self.scratch("yb_d", [NT, 128, 2048])
        self.xm1 = self.scratch("xm1", [NTOK, D])
        self.b_rec = [Buf() for _ in range(NT)]
        self.b_yb = [Buf() for _ in range(NT)]
        self.b_xm1 = [Buf() for _ in range(NT)]
        self.b_out = [Buf() for _ in range(NT)]
        self.xm0 = self.scratch("xm0", [NTOK, D])
        self.xb = self.scratch("xb", [NTOK, D])
        self.q_d = self.scratch("q_d", [NT, 128, 6, 128], BF16)
        self.yp_d = self.scratch("yp_d", [NT, 128, 2, 128], BF16)
        final_name = {"f0": "xb", "l0": "xm0", "l1": "xm1"}.get(self.phases[-1], "out")
        self.final_name = final_name
        if final_name == "out":
            self.out = nc.dram_tensor("out", [T, D], F32, kind="ExternalOutput").ap()

        self.b_xa = [Buf() for _ in range(NT)]
        self.b_xm0 = [Buf() for _ in range(NT)]
        self.b_xb = [Buf() for _ in range(NT)]
        self.b_q = [Buf() for _ in range(NT)]
        self.b_yp = [Buf() for _ in range(NT)]

        with ExitStack() as st:
            self.S = Sched(nc, st, strict=STRICT)
            self.glob(st)
            for ph in self.phases:
                getattr(self, "ph_" + ph)()
                self.S.barrier()
                self.S.mark()
            self.S.finish()
            self.S.emit()
        return nc

    def glob(self, st):
        sb = lambda n, s, d=F32: self.sb(st, n, s, d)
        self.idf = sb("idf", [128, 128])
        self.idb = sb("idb", [128, 128], BF16)
        self.ones = sb("ones", [128, 128])
        self.mod = [sb(f"mod{i}", [128, 48, 2]) for i in range(2)]
        self.nws = sb("nws", [128, 2, 4, 8])
        self.A1 = [[sb(f"A1_{i}{s}", [128, 8]) for s in range(2)] for i in range(2)]
        self.A2 = [[sb(f"A2_{i}{s}", [128, 8]) for s in range(2)] for i in range(2)]
        self.G1 = [[sb(f"G1_{i}{s}", [128, 8]) for s in range(2)] for i in range(2)]
        self.G2 = [[sb(f"G2_{i}{s}", [128, 8]) for s in range(2)] for i in range(2)]
        S = self.S
        S.dma("sp", self.idf[:], self.ident, writes=[self.idf])
        S.dma("sp", self.nws[:], self.nw, writes=[self.nws])
        self.cp("dve", self.idb[:], self.idf[:], [self.idf], [self.idb])
        self.memset("dve", self.ones[:], 1.0, [self.ones])

    def ph_p0(self):
        S = self.S
        with ExitStack() as st:
            sb = lambda n, s, d=F32: self.sb(st, n, s, d)
            cond = sb("cond", [128, 8, 2])
            scond = sb("scond", [128, 8, 2])
            adab = sb("adab", [128, 48])
            wst = [sb(f"wst{j}", [128, 8, 768]) for j in range(2)]
            pm = self.ps(st, "pm", [128, 256, 2])
            S.dma("sp", cond[:], self.cond, writes=[cond])
            self.act(scond[:], cond[:], AF.Silu, [cond], [scond])
            for i in range(2):
                S.dma("sp", adab[:], self.ada_b[i], writes=[adab])
                for cg in range(8):
                    w = wst[cg % 2]
                    S.dma("sp", w[:], self.ada_w[i][:, cg * 768:(cg + 1) * 768].rearrange("(k p) n -> p k n", p=128),
                          writes=[w])
                    for cc in range(6):
                        c = cg * 6 + cc
                        items = [(pm[:, c, :], w[:, k, cc * 128:(cc + 1) * 128], scond[:, k, :], k == 0, k == 7)
                                 for k in range(8)]
                        self.mm(items, [w, scond], [pm])
                md = self.mod[i]
                if "modd" in self.dbg:
                    if i == 0:
                        self.modd = self.scratch("modd", [2, 128, 96])
                self.tt("dve", md[:], pm[:, 0:48, :], adab[:].unsqueeze(2).to_broadcast([128, 48, 2]), ALU.add,
                        [pm, adab], [md])
                for s in range(2):
                    nws = self.nws
                    self.stt(self.A1[i][s][:], md[:, 8:16, s], 1.0, nws[:, i, 0, :], ALU.add, ALU.mult,
                             [md, nws], [self.A1[i][s]])
                    self.stt(self.A2[i][s][:], md[:, 32:40, s], 1.0, nws[:, i, 2, :], ALU.add, ALU.mult,
                             [md, nws], [self.A2[i][s]])
                    self.tt("dve", self.G1[i][s][:], md[:, 16:24, s], nws[:, i, 1, :], ALU.mult,
                            [md, nws], [self.G1[i][s]])
                    self.tt("dve", self.G2[i][s][:], md[:, 40:48, s], nws[:, i, 3, :], ALU.mult,
                            [md, nws], [self.G2[i][s]])
                if "modd" in self.dbg:
                    S.dma("sp", self.modd[i], md[:].rearrange("p a b -> p (a b)"), reads=[md], writes=[Buf()])

    def bcast_rows(self, st_tmp, pB, srcs, dsts):
        tmp = [self.sb(st_tmp, f"bct{j}", [128, 128]) for j in range(2)]
        n = 0
        for src, dst in zip(srcs, dsts):
            for k in range(8):
                t = tmp[n % 2]
                n += 1
                self.ts("dve", t[:], self.ones[:], src[:, k:k + 1], None, ALU.mult, None, [self.ones, src], [t])
                self.mm([(pB[:, k * 128:(k + 1) * 128], t[:], self.idf[:], True, True)], [t, self.idf], [pB])
            self.cp("act", dst[:], pB[:], [pB], [dst])

    def load_w_bf16(self, dst, src_dram, kchunks, ncols, step=512):
        for c0 in range(0, ncols, step):
            c1 = min(ncols, c0 + step)
            for k0 in range(0, kchunks, 8):
                k1 = min(kchunks, k0 + 8)
                self.S.dma("pool", dst[:, k0:k1, c0:c1],
                           src_dram[k0 * 128:k1 * 128, c0:c1].rearrange("(k p) n -> p k n", p=128),
                           writes=[dst])

    def norm_tile(self, xt, st_, A, B, hT_out, hT_tl, col0, scr):
        (sq, ss, xn, pT) = scr
        self.rstd(xt[:], D, sq[:], ss, [xt], [sq])
        self.ts("dve", xn[:], xt[:], ss[:, 0:1], None, ALU.mult, None, [xt, ss], [xn])
        self.trn([(pT[:, k, :], xn[:, k * 128:(k + 1) * 128], self.idb[:]) for k in range(8)],
                 [xn, self.idb], [pT])
        for k in range(8):
            self.act(hT_out[:, k, col0:col0 + 128], pT[:, k, :], AF.Identity, [pT, A, B], [hT_tl],
                     scale=A[:, k:k + 1], bias=B[:, k:k + 1])

    def ph_l0(self):
        S = self.S
        i_layer = 0
        with ExitStack() as st0:
            sb0 = lambda n, s, d=F32: self.sb(st0, n, s, d)
            kT = sb0("kT", [128, 2, NTOK], BF16)
            vA = sb0("vA", [128, NT, 2, 129], BF16)
            Gbc = [sb0(f"G1bc{s}", [128, D]) for s in range(2)]
            self.memset("pool", vA[:, :, :, 128:129], 1.0, [vA])
            with ExitStack() as st:
                sb = lambda n, s, d=F32: self.sb(st, n, s, d)
                ps = lambda n, s, d=F32: self.ps(st, n, s, d)
                w_in = sb("w_in", [128, 8, 1536], BF16)
                self.load_w_bf16(w_in, self.w_in0, 8, 1536)
                bands = sb("bands", [128, 20, 128])
                S.dma("sp", bands[:], self.bands, writes=[bands])
                pwb_f = sb("pwb_f", [128, 2, 128])
                pwb = sb("pwb", [128, 2, 128], BF16)
                S.dma("sp", pwb_f[:], self.pwblk, writes=[pwb_f])
                self.cp("dve", pwb[:], pwb_f[:], [pwb_f], [pwb])
                psc = sb("psc", [128, 2])
                S.dma("sp", psc[:], self.pscale, writes=[psc])
                qkg = sb("qkg", [128, 2, 128])
                S.dma("sp", qkg[:], self.qkg, writes=[qkg])
                xt = [sb(f"xt{j}", [128, D]) for j in range(2)]
                cs = [sb(f"cs{j}", [128, 2, 64]) for j in range(2)]
                sq = sb("sq", [128, D], BF16)
                ss = sb("ss", [128, 1])
                xn = sb("xn", [128, D], BF16)
                hT = sb("hT", [128, 8, 128], BF16)
                a_sb = [sb(f"a{j}", [128, 256]) for j in range(4)]
                qsq = sb("qsq", [128, 8, 128], BF16)
                ssq = sb("ssq", [128, 8])
                qkn = sb("qkn", [128, 8, 128])
                t1 = sb("t1", [128, 8, 64]); t2 = sb("t2", [128, 8, 64])
                t3 = sb("t3", [128, 8, 64]); t4 = sb("t4", [128, 8, 64])
                qr = sb("qr", [128, 8, 128], BF16)
                qst = [sb(f"qst{j}", [128, 6, 128], BF16) for j in range(2)]
                p_sb = sb("p_sb", [128, 256])
                pT_sb = sb("pT_sb", [128, 2, 128], BF16)
                ypst = [sb(f"ypst{j}", [128, 2, 128], BF16) for j in range(2)]
                pT = ps("pT", [128, 8, 128], BF16)
                pu = [ps(f"pu{j}", [128, 512]) for j in range(3)]
                pQ = ps("pQ", [128, 8, 128], BF16)
                pp = ps("pp", [128, 512])
                ptr = ps("ptr", [128, 4, 128])
                py = ps("py", [128, 4, 128])

                def pool_tile(j):
                    s = 0 if j < 2 else 1
                    first = (j == 0) or (j == 2)
                    last = (j == 1) or (j == NT1 - 1)
                    for g in range(4):
                        items = []
                        nbs = []
                        if not first:
                            nbs.append((j - 1, 0))
                        nbs.append((j, 3 if first else (4 if last else 1)))
                        if not last:
                            nbs.append((j + 1, 2))
                        for n_i, (jj, kind) in enumerate(nbs):
                            items.append((pp[:, g * 64:(g + 1) * 64], bands[:, g * 5 + kind, :],
                                          a_sb[jj % 4][:, g * 64:(g + 1) * 64], n_i == 0, n_i == len(nbs) - 1))
                        self.mm(items, [bands] + [a_sb[jj % 4] for jj, _ in nbs], [pp])
                    self.cp("act", p_sb[:], pp[:, 0:256], [pp], [p_sb])
                    self.trn([(ptr[:, c, :], p_sb[:, c * 128:(c + 1) * 128], self.idf[:]) for c in range(2)],
                             [p_sb, self.idf], [ptr])
                    self.cp("dve", pT_sb[:], ptr[:, 0:2, :], [ptr], [pT_sb])
                    self.mm([(py[:, c, :], pwb[:, c, :], pT_sb[:, c, :], True, True) for c in range(2)],
                            [pwb, pT_sb], [py])
                    yp = ypst[j % 2]
                    for c in range(2):
                        self.act(yp[:, c, :], py[:, c, :], AF.Identity, [py, psc], [yp], scale=psc[:, c:c + 1])
                    S.dma(STQ, self.yp_d[j], yp[:], reads=[yp], writes=[self.b_yp[j]])

                NT1 = self.nt1 or NT
                for i in range(NT1):
                    S.mark(4000)
                    s = 0 if i >= 2 else 1
                    x_t = xt[i % 2]
                    S.dma("sp", x_t[:], self.xa[i * 128:(i + 1) * 128, :], reads=[self.b_xa[i]], writes=[x_t])
                    if i >= 2:
                        c_t = cs[i % 2]
                        S.dma("sp", c_t[:, 0, :], self.cos[(i - 2) * 128:(i - 1) * 128, :], writes=[c_t])
                        S.dma("sp", c_t[:, 1, :], self.sin[(i - 2) * 128:(i - 1) * 128, :], writes=[c_t])
                    self.rstd(x_t[:], D, sq[:], ss, [x_t], [sq])
                    self.ts("dve", xn[:], x_t[:], ss[:, 0:1], None, ALU.mult, None, [x_t, ss], [xn])
                    self.trn([(pT[:, k, :], xn[:, k * 128:(k + 1) * 128], self.idb[:]) for k in range(8)],
                             [xn, self.idb], [pT])
                    A = self.A1[0][s]
                    md = self.mod[0]
                    for k in range(8):
                        self.act(hT[:, k, :], pT[:, k, :], AF.Identity, [pT, A, md], [hT],
                                 scale=A[:, k:k + 1], bias=md[:, k, s:s + 1])
                    for n in range(3):
                        self.mm([(pu[n][:], hT[:, k, :], w_in[:, k, n * 512:(n + 1) * 512], k == 0, k == 7)
                                 for k in range(8)], [hT, w_in], [pu[n]])
                    a_t = a_sb[i % 4]
                    self.cp("act", a_t[:], pu[0][:, 0:256], [pu[0]], [a_t])
                    self.cp("act", vA[:, i, :, 0:128], pu[2][:, 256:512].rearrange("p (h d) -> p h d", h=2),
                            [pu[2]], [vA])
                    pieces = [(pu[0][:, 256:512], 0, 2), (pu[1][:, 0:512], 2, 4), (pu[2][:, 0:256], 6, 2)]
                    for (src, h0, nh) in pieces:
                        self.act(qsq[:, h0:h0 + nh, :], src.rearrange("p (h d) -> p h d", h=nh), AF.Square,
                                 [pu[0], pu[1], pu[2]], [qsq])
                    self.S.op("dve", lambda e: e.tensor_reduce(out=ssq[:], in_=qsq[:], axis=AX.X, op=ALU.add),
                              [qsq], [ssq])
                    self.ts("dve", ssq[:], ssq[:], 1.0 / 128, EPS, ALU.mult, ALU.add, [ssq], [ssq])
                    self.act(ssq[:], ssq[:], AF.Sqrt, [ssq], [ssq])
                    self.S.op("dve", lambda e: e.reciprocal(out=ssq[:], in_=ssq[:]), [ssq], [ssq])
                    for (src, h0, nh) in pieces:
                        for hh in range(nh):
                            h = h0 + hh
                            gsel = 0 if h < 6 else 1
                            self.stt(qkn[:, h, :], src[:, hh * 128:(hh + 1) * 128], ssq[:, h:h + 1],
                                     qkg[:, gsel, :], ALU.mult, ALU.mult, [pu[0], pu[1], pu[2], ssq, qkg], [qkn])
                    if i >= 2:
                        c_t = cs[i % 2]
                        cosb = c_t[:, 0, :].unsqueeze(1).to_broadcast([128, 8, 64])
                        sinb = c_t[:, 1, :].unsqueeze(1).to_broadcast([128, 8, 64])
                        x1 = qkn[:, :, 0::2]
                        x2 = qkn[:, :, 1::2]
                        self.tt("dve", t1[:], x1, cosb, ALU.mult, [qkn, c_t], [t1])
                        self.tt("dve", t2[:], x2, sinb, ALU.mult, [qkn, c_t], [t2])
                        self.tt("dve", qr[:, :, 0::2], t1[:], t2[:], ALU.subtract, [t1, t2], [qr])
                        self.tt("pool", t3[:], x1, sinb, ALU.mult, [qkn, c_t], [t3])
                        self.tt("pool", t4[:], x2, cosb, ALU.mult, [qkn, c_t], [t4])
                        self.tt("pool", qr[:, :, 1::2], t3[:], t4[:], ALU.add, [t3, t4], [qr])
                    else:
                        self.cp("dve", qr[:], qkn[:], [qkn], [qr])
                    self.trn([(pQ[:, h, :], qr[:, h, :], self.idb[:]) for h in range(8)], [qr, self.idb], [pQ])
                    q_s = qst[i % 2]
                    self.cp("act", q_s[:], pQ[:, 0:6, :], [pQ], [q_s])
                    self.cp("dve", kT[:, :, i * 128:(i + 1) * 128], pQ[:, 6:8, :], [pQ], [kT])
                    S.dma(STQ, self.q_d[i], q_s[:], reads=[q_s], writes=[self.b_q[i]])
                    if i == 1:
                        pool_tile(0)
                        pool_tile(1)
                    elif i >= 3:
                        pool_tile(i - 1)
                        if i == NT1 - 1:
                            pool_tile(i)
            S.barrier()
            with ExitStack() as st:
                sb = lambda n, s, d=F32: self.sb(st, n, s, d)
                ps = lambda n, s, d=F32: self.ps(st, n, s, d)
                wout = sb("wout", [128, 8, D], BF16)
                self.load_w_bf16(wout, self.w_out0, 8, D)
                qblk = [sb(f"qblk{j}", [128, 4, 6, 128], BF16) for j in range(2)]
                pex = [sb(f"pex{j}", [128, 512], BF16) for j in range(3)]
                o_sb = sb("o_sb", [128, 4, 768])
                rsum = sb("rsum", [128, 4])
                mixT = [sb(f"mixT{j}", [128, 8, 512], BF16) for j in range(2)]
                xres = [sb(f"xres{j}", [128, D]) for j in range(2)]
                tmp = sb("tmp", [128, D])
                xo = [sb(f"xo{j}", [128, D]) for j in range(2)]
                sq = sb("sq", [128, D], BF16)
                ss = sb("ss", [128, 1])
                pS = [ps(f"pS{j}", [128, 512]) for j in range(2)]
                pO = [ps(f"pO{j}", [128, 512]) for j in range(4)]
                pY = ps("pY", [128, D])
                self.bcast_rows(st, pY, [self.G1[0][0], self.G1[0][1]], Gbc)
                scale = 128.0 ** -0.5
                blocks = [([0, 1], [0, 1], 1)] + [([2 + 4 * b + j for j in range(4)], list(range(self.nt1 or NT)), 0)
                                                  for b in range(T // 512)]
                if self.nblk is not None:
                    blocks = blocks[:self.nblk]
                for bi, (qtiles, ktiles, s) in enumerate(blocks):
                    S.mark(4000)
                    nsub = len(qtiles)
                    nq = nsub * 128
                    qb = qblk[bi % 2]
                    mx = mixT[bi % 2]
                    for j, qt in enumerate(qtiles):
                        S.dma("sp", qb[:, j, :, :], self.q_d[qt], reads=[self.b_q[qt]], writes=[qb])
                        S.dma("sp", mx[:, 0:2, j * 128:(j + 1) * 128], self.yp_d[qt], reads=[self.b_yp[qt]],
                              writes=[mx])
                    steps = [(h, kt) for h in range(6) for kt in ktiles]
                    nstep = len(steps)

                    def qk(si):
                        h, kt = steps[si]
                        self.mm([(pS[si % 2][:, 0:nq].rearrange("p (a b) -> p a b", b=128), kT[:, h // 3, kt * 128:(kt + 1) * 128],
                                  qb[:, 0:nsub, h, :], True, True)], [kT, qb], [pS[si % 2]])
                    qk(0)
                    for si in range(nstep):
                        h, kt = steps[si]
                        if si + 1 < nstep:
                            qk(si + 1)
                        pe_ = pex[si % 3]
                        self.act(pe_[:, 0:nq], pS[si % 2][:, 0:nq], AF.Exp, [pS[si % 2]], [pe_], scale=scale)
                        first = kt == ktiles[0]
                        lastk = kt == ktiles[-1]
                        self.mm([(pO[sub][:, 0:129], pe_[:, sub * 128:(sub + 1) * 128], vA[:, kt, h // 3, :],
                                  first, lastk) for sub in range(nsub)], [pe_, vA], [pO[sub] for sub in range(nsub)])
                        if lastk:
                            for sub in range(nsub):
                                self.S.op("dve", lambda e, sub=sub: e.reciprocal(out=rsum[:, sub:sub + 1],
                                                                                 in_=pO[sub][:, 128:129]),
                                          [pO[sub]], [rsum])
                                self.ts("dve", o_sb[:, sub, h * 128:(h + 1) * 128], pO[sub][:, 0:128],
                                        rsum[:, sub:sub + 1], None, ALU.mult, None, [pO[sub], rsum], [o_sb])
                    for sub in range(nsub):
                        for hb in range(0, 6, 4):
                            hs = list(range(hb, min(6, hb + 4)))
                            pst = pS[(sub * 2 + hb // 4) % 2]
                            self.trn([(pst[:, jj * 128:(jj + 1) * 128], o_sb[:, sub, h * 128:(h + 1) * 128],
                                       self.idf[:]) for jj, h in enumerate(hs)], [o_sb, self.idf], [pst])
                            self.cp("act", mx[:, 2 + hb:2 + hb + len(hs), sub * 128:(sub + 1) * 128],
                                    pst[:, 0:len(hs) * 128].rearrange("p (h d) -> p h d", h=len(hs)), [pst], [mx])
                    for sub, qt in enumerate(qtiles):
                        xr = xres[sub % 2]
                        S.dma("sp", xr[:], self.xa[qt * 128:(qt + 1) * 128, :], reads=[self.b_xa[qt]], writes=[xr])
                        for n in range(2):
                            self.mm([(pY[:, n * 512:(n + 1) * 512], mx[:, k, sub * 128:(sub + 1) * 128],
                                      wout[:, k, n * 512:(n + 1) * 512], k == 0, k == 7) for k in range(8)],
                                    [mx, wout], [pY])
                        self.rstd(pY[:], D, sq[:], ss, [pY], [sq])
                        self.stt(tmp[:], pY[:], ss[:, 0:1], Gbc[s][:], ALU.mult, ALU.mult, [pY, ss, Gbc[s]], [tmp])
                        x_o = xo[sub % 2]
                        self.tt("pool", x_o[:], tmp[:], xr[:], ALU.add, [tmp, xr], [x_o])
                        S.dma(STQ, self.xm0[qt * 128:(qt + 1) * 128, :], x_o[:], reads=[x_o],
                              writes=[self.b_xm0[qt]])

    def ffn(self, li, src, b_src, dst, b_dst, dst_off, tiles_blocks):
        S = self.S
        with ExitStack() as st:
            sb = lambda n, s, d=F32: self.sb(st, n, s, d)
            ps = lambda n, s, d=F32: self.ps(st, n, s, d)
            wup = sb("wup", [128, 8, 2 * DFF], BF16)
            wdn = sb("wdn", [128, NFF, D], BF16)
            self.load_w_bf16(wup, self.w_up[li], 8, 2 * DFF)
            self.load_w_bf16(wdn, self.w_dn[li], NFF, D)
            cw = sb("cw", [128, 2 * NFF, 3])
            cb = sb("cb", [128, 2 * NFF])
            S.dma("sp", cw[:], self.fcw[li], writes=[cw])
            S.dma("sp", cb[:], self.fcb[li], writes=[cb])
            streams = sorted(set(s for _, s in tiles_blocks))
            Gbc = {s: sb(f"G2bc{s}", [128, D]) for s in streams}
            hTb = [sb(f"hTb{j}", [128, 8, 258], BF16) for j in range(2)]
            xt = [sb(f"xt{j}", [128, D]) for j in range(2)]
            xres = [sb(f"xres{j}", [128, D]) for j in range(2)]
            sq = sb("sq", [128, D], BF16)
            ss = sb("ss", [128, 1])
            xn = sb("xn", [128, D], BF16)
            cv = [sb(f"cv{j}", [128, 256]) for j in range(8)]
            sg = [sb(f"sg{j}", [128, 256]) for j in range(2)]
            gT = sb("gT", [128, NFF, 256], BF16)
            tmp = sb("tmp", [128, D])
            xo = [sb(f"xo{j}", [128, D]) for j in range(2)]
            pT = ps("pT", [128, 8, 128], BF16)
            pU = [ps(f"pU{j}", [128, 512]) for j in range(4)]
            pF = ps("pF", [128, D])
            self.bcast_rows(st, pF, [self.G2[li][s] for s in streams], [Gbc[s] for s in streams])
            nb = len(tiles_blocks)

            def stageA(j):
                t0, s = tiles_blocks[j]
                hb = hTb[j % 2]
                A = self.A2[li][s]
                md = self.mod[li]
                for tt_ in range(2):
                    ti = t0 + tt_
                    x_t = xt[tt_]
                    S.dma("sp", x_t[:], src[ti * 128:(ti + 1) * 128, :], reads=[b_src[ti]], writes=[x_t])
                    self.rstd(x_t[:], D, sq[:], ss, [x_t], [sq])
                    self.ts("dve", xn[:], x_t[:], ss[:, 0:1], None, ALU.mult, None, [x_t, ss], [xn])
                    self.trn([(pT[:, k, :], xn[:, k * 128:(k + 1) * 128], self.idb[:]) for k in range(8)],
                             [xn, self.idb], [pT])
                    for k in range(8):
                        self.act(hb[:, k, 1 + tt_ * 128:1 + (tt_ + 1) * 128], pT[:, k, :], AF.Identity,
                                 [pT, A, md], [hb], scale=A[:, k:k + 1], bias=md[:, 24 + k, s:s + 1])

            def halos(j):
                t0, s = tiles_blocks[j]
                hb = hTb[j % 2]
                cont = j > 0 and tiles_blocks[j - 1][1] == s and tiles_blocks[j - 1][0] + 2 == t0
                if cont:
                    hp = hTb[(j - 1) % 2]
                    self.cp("pool", hb[:, :, 0:1], hp[:, :, 256:257], [hp], [hb])
                    self.cp("pool", hp[:, :, 257:258], hb[:, :, 1:2], [hb], [hp])
                else:
                    self.memset("pool", hb[:, :, 0:1], 0.0, [hb])
                    if j > 0:
                        hp = hTb[(j - 1) % 2]
                        self.memset("pool", hp[:, :, 257:258], 0.0, [hp])

            def conv(pu_, c, t_a, t_b):
                self.act(t_a[:], pu_[:, 1:257], AF.Identity, [pu_, cw, cb], [t_a],
                         scale=cw[:, c, 1:2], bias=cb[:, c:c + 1])
                self.stt(t_b[:], pu_[:, 0:256], cw[:, c, 0:1], t_a[:], ALU.mult, ALU.add, [pu_, cw, t_a], [t_b])
                self.stt(t_a[:], pu_[:, 2:258], cw[:, c, 2:3], t_b[:], ALU.mult, ALU.add, [pu_, cw, t_b], [t_a])

            def stageB(j):
                t0, s = tiles_blocks[j]
                hb = hTb[j % 2]
                def front(cc):
                    puv = pU[(cc % 2) * 2]
                    pug = pU[(cc % 2) * 2 + 1]
                    cvA = cv[(cc % 2) * 4:(cc % 2) * 4 + 4]
                    self.mm([(puv[:, 0:258], wup[:, k, cc * 128:(cc + 1) * 128], hb[:, k, :], k == 0, k == 7)
                             for k in range(8)], [wup, hb], [puv])
                    self.mm([(pug[:, 0:258], wup[:, k, DFF + cc * 128:DFF + (cc + 1) * 128], hb[:, k, :],
                              k == 0, k == 7) for k in range(8)], [wup, hb], [pug])
                    conv(puv, cc, cvA[0], cvA[1])
                    conv(pug, NFF + cc, cvA[2], cvA[3])

                def back(cc):
                    cvA = cv[(cc % 2) * 4:(cc % 2) * 4 + 4]
                    sg_ = sg[cc % 2]
                    self.act(sg_[:], cvA[2][:], AF.Silu, [cvA[2]], [sg_])
                    self.tt("pool", gT[:, cc, :], sg_[:], cvA[0][:], ALU.mult, [sg_, cvA[0]], [gT])

                for cc in range(NFF):
                    front(cc)
                    if cc > 0:
                        back(cc - 1)
                back(NFF - 1)
                for tt_ in range(2):
                    ti = t0 + tt_
                    xr = xres[tt_]
                    S.dma("sp", xr[:], src[ti * 128:(ti + 1) * 128, :], reads=[b_src[ti]], writes=[xr])
                    for n in range(2):
                        self.mm([(pF[:, n * 512:(n + 1) * 512], gT[:, cc, tt_ * 128:(tt_ + 1) * 128],
                                  wdn[:, cc, n * 512:(n + 1) * 512], cc == 0, cc == NFF - 1) for cc in range(NFF)],
                                [gT, wdn], [pF])
                    self.rstd(pF[:], D, sq[:], ss, [pF], [sq])
                    self.stt(tmp[:], pF[:], ss[:, 0:1], Gbc[s][:], ALU.mult, ALU.mult, [pF, ss, Gbc[s]], [tmp])
                    x_o = xo[tt_]
                    self.tt("pool", x_o[:], tmp[:], xr[:], ALU.add, [tmp, xr], [x_o])
                    r0 = ti * 128 - dst_off
                    S.dma(STQ, dst[r0:r0 + 128, :], x_o[:], reads=[x_o], writes=[b_dst[ti]])

            stageA(0)
            halos(0)
            for j in range(nb):
                S.mark(4000)
                if j + 1 < nb:
                    stageA(j + 1)
                    halos(j + 1)
                else:
                    self.memset("pool", hTb[j % 2][:, :, 257:258], 0.0, [hTb[j % 2]])
                stageB(j)

    def ph_f0(self):
        nt1 = self.nt1 or NT
        blocks = [(0, 1)] + [(2 + 2 * j, 0) for j in range((nt1 - 2) // 2)]
        self.ffn(0, self.xm0, self.b_xm0, self.xb, self.b_xb, 0, blocks)


    def ph_f1(self):
        nt1 = self.nt1 or NT
        blocks = [(2 + 2 * j, 0) for j in range((nt1 - 2) // 2)]
        self.ffn(1, self.xm1, self.b_xm1, self.out, self.b_out, 256, blocks)

    def ph_l1(self):
        self.l1_prep()
        self.S.barrier()
        nt1 = self.nt1 or NT
        self.scan_pass(1, [1, 0] + list(range(nt1 - 1, 1, -1)))
        self.S.barrier()
        self.scan_pass(0, list(range(nt1)))

    def l1_prep(self):
        S = self.S
        li = 1
        W_ = 259
        nt1 = self.nt1 or NT
        blocks = [(0, 1)] + [(2 + 2 * j, 0) for j in range((nt1 - 2) // 2)]
        with ExitStack() as st:
            sb = lambda n, s, d=F32: self.sb(st, n, s, d)
            ps = lambda n, s, d=F32: self.ps(st, n, s, d)
            w_in = sb("w_in1", [128, 8, 5184], BF16)
            self.load_w_bf16(w_in, self.w_in1, 8, 5184)
            cw = sb("scw", [128, 24, 4]); cb = sb("scb", [128, 24]); dtb = sb("dtb", [128, 64])
            S.dma("sp", cw[:], self.scw, writes=[cw])
            S.dma("sp", cb[:], self.scb, writes=[cb])
            S.dma("sp", dtb[:], self.dtb, writes=[dtb])
            hTb = [sb(f"hTb{j}", [128, 8, W_], BF16) for j in range(2)]
            xt = [sb(f"xt{j}", [128, D]) for j in range(2)]
            sq = sb("sq", [128, D], BF16); ss = sb("ss", [128, 1]); xn = sb("xn", [128, D], BF16)
            cva = [sb(f"cva{j}", [128, 256]) for j in range(2)]
            cvb = [sb(f"cvb{j}", [128, 256]) for j in range(2)]
            xbcT = sb("xbcT", [128, 24, 256], BF16)
            rec = [sb(f"rec{j}", [128, 4608], BF16) for j in range(2)]
            dtt = [sb(f"dtt{j}", [128, 64]) for j in range(2)]
            pT = ps("pT", [128, 8, 128], BF16)
            pU = [ps(f"pU{j}", [128, 512]) for j in range(2)]
            pTt = ps("pTt", [128, 8, 128], BF16)
            pZ = ps("pZ", [128, 512])
            pD = ps("pD", [128, 512])
            nb = len(blocks)

            def stageA(j):
                t0, s = blocks[j]
                hb = hTb[j % 2]
                A = self.A1[li][s]
                md = self.mod[li]
                for tt_ in range(2):
                    ti = t0 + tt_
                    x_t = xt[tt_]
                    S.dma("sp", x_t[:], self.xb[ti * 128:(ti + 1) * 128, :], reads=[self.b_xb[ti]], writes=[x_t])
                    self.rstd(x_t[:], D, sq[:], ss, [x_t], [sq])
                    self.ts("dve", xn[:], x_t[:], ss[:, 0:1], None, ALU.mult, None, [x_t, ss], [xn])
                    self.trn([(pT[:, k, :], xn[:, k * 128:(k + 1) * 128], self.idb[:]) for k in range(8)],
                             [xn, self.idb], [pT])
                    for k in range(8):
                        self.act(hb[:, k, 2 + tt_ * 128:2 + (tt_ + 1) * 128], pT[:, k, :], AF.Identity,
                                 [pT, A, md], [hb], scale=A[:, k:k + 1], bias=md[:, k, s:s + 1])

            def halos(j):
                t0, s = blocks[j]
                hb = hTb[j % 2]
                cont = j > 0 and blocks[j - 1][1] == s and blocks[j - 1][0] + 2 == t0
                if cont:
                    hp = hTb[(j - 1) % 2]
                    self.cp("pool", hb[:, :, 0:2], hp[:, :, 256:258], [hp], [hb])
                    self.cp("pool", hp[:, :, 258:259], hb[:, :, 2:3], [hb], [hp])
                else:
                    self.memset("pool", hb[:, :, 0:2], 0.0, [hb])
                    if j > 0:
                        hp = hTb[(j - 1) % 2]
                        self.memset("pool", hp[:, :, 258:259], 0.0, [hp])

            def stageB(j):
                t0, s = blocks[j]
                hb = hTb[j % 2]
                def front(c):
                    pu_ = pU[c % 2]
                    ta = cva[c % 2]
                    tb = cvb[c % 2]
                    self.mm([(pu_[:, 0:W_], w_in[:, k, 2048 + c * 128:2048 + (c + 1) * 128], hb[:, k, :],
                              k == 0, k == 7) for k in range(8)], [w_in, hb], [pu_])
                    self.act(ta[:], pu_[:, 2:258], AF.Identity, [pu_, cw, cb], [ta],
                             scale=cw[:, c, 2:3], bias=cb[:, c:c + 1])
                    self.stt(tb[:], pu_[:, 0:256], cw[:, c, 0:1], ta[:], ALU.mult, ALU.add, [pu_, cw, ta], [tb])
                    self.stt(ta[:], pu_[:, 1:257], cw[:, c, 1:2], tb[:], ALU.mult, ALU.add, [pu_, cw, tb], [ta])
                    self.stt(tb[:], pu_[:, 3:259], cw[:, c, 3:4], ta[:], ALU.mult, ALU.add, [pu_, cw, ta], [tb])

                def back(c):
                    self.act(xbcT[:, c, :], cvb[c % 2][:], AF.Silu, [cvb[c % 2]], [xbcT])

                for c in range(24):
                    front(c)
                    if c > 0:
                        back(c - 1)
                back(23)
                for tt_ in range(2):
                    ti = t0 + tt_
                    rt = rec[tt_]
                    for g0 in (0, 8, 16):
                        n = 8 if g0 < 16 else 4
                        self.trn([(pTt[:, j2, :], xbcT[:, g0 + j2, tt_ * 128:(tt_ + 1) * 128], self.idb[:])
                                  for j2 in range(n)], [xbcT, self.idb], [pTt])
                        self.cp("act", rt[:, g0 * 128:(g0 + n) * 128].rearrange("p (a b) -> p a b", b=128),
                                pTt[:, 0:n, :], [pTt], [rt])
                    for n in range(4):
                        self.mm([(pZ[:], hb[:, k, 2 + tt_ * 128:2 + (tt_ + 1) * 128],
                                  w_in[:, k, n * 512:(n + 1) * 512], k == 0, k == 7) for k in range(8)],
                                [hb, w_in], [pZ])
                        self.act(rt[:, 2560 + n * 512:2560 + (n + 1) * 512], pZ[:], AF.Silu, [pZ], [rt])
                    self.mm([(pD[:, 0:64], hb[:, k, 2 + tt_ * 128:2 + (tt_ + 1) * 128], w_in[:, k, 5120:5184],
                              k == 0, k == 7) for k in range(8)], [hb, w_in], [pD])
                    d_t = dtt[tt_]
                    self.tt("dve", d_t[:], pD[:, 0:64], dtb[:], ALU.add, [pD, dtb], [d_t])
                    self.act(d_t[:], d_t[:], AF.Exp, [d_t], [d_t])
                    self.act(d_t[:], d_t[:], AF.Ln, [d_t], [d_t], bias=1.0)
                    S.dma(STQ, self.dt_d[ti], d_t[:], reads=[d_t], writes=[self.b_rec[ti]])
                    S.dma(STQ, self.rec_tok[ti], rt[:], reads=[rt], writes=[self.b_rec[ti]])
                    S.dma(STQ, self.rec_fm[ti], xbcT[:, 16:24, tt_ * 128:(tt_ + 1) * 128], reads=[xbcT],
                          writes=[self.b_rec[ti]])

            stageA(0)
            halos(0)
            for j in range(nb):
                S.mark(4000)
                if j + 1 < nb:
                    stageA(j + 1)
                    halos(j + 1)
                else:
                    self.memset("pool", hTb[j % 2][:, :, 258:259], 0.0, [hTb[j % 2]])
                stageB(j)

    def scan_pass(self, d, order):
        S = self.S
        li = 1
        final = d == 0
        with ExitStack() as st:
            sb = lambda n, s, dt_=F32: self.sb(st, n, s, dt_)
            ps = lambda n, s, dt_=F32: self.ps(st, n, s, dt_)
            S_f = sb("S_f", [128, 2048]); S_b = sb("S_b", [128, 2048], BF16)
            self.memset("dve", S_f[:], 0.0, [S_f])
            self.memset("dve", S_b[:], 0.0, [S_b])
            msk = sb("msk", [128, 4, 128])
            S.dma("sp", msk[:], self.msk, writes=[msk])
            Md = msk[:, d, :]
            Neg = msk[:, 2 + d, :]
            Abc = sb("Abc", [128, 64])
            S.dma("sp", Abc[:], self.alog, writes=[Abc])
            self.act(Abc[:], Abc[:], AF.Exp, [Abc], [Abc])
            self.ts("dve", Abc[:], Abc[:], -1.0, None, ALU.mult, None, [Abc], [Abc])
            rec = [sb(f"rec{j}", [128, 4608], BF16) for j in range(2)]
            recf = [sb(f"recf{j}", [128, 8, 128], BF16) for j in range(2)]
            dtt = [sb(f"dtt{j}", [128, 64]) for j in range(2)]
            a_t = sb("a_t", [128, 32]); acum = sb("acum", [128, 32]); nacum = sb("nacum", [128, 32])
            tot = sb("tot", [128, 32]); E_t = sb("E_t", [128, 32]); dte = sb("dte", [128, 32])
            etot = sb("etot", [128, 32])
            xdt = sb("xdt", [128, 2048], BF16); xdt2 = sb("xdt2", [128, 2048], BF16)
            rhs4 = [sb(f"rhs4{j}", [128, 4, 128]) for j in range(2)]
            Dp = [sb(f"Dp{j}", [128, 4, 128]) for j in range(2)]
            Lt = [sb(f"Lt{j}", [128, 4, 128]) for j in range(2)]
            Wt = [sb(f"Wt{j}", [128, 4, 128], BF16) for j in range(2)]
            cb_sb = sb("cb_sb", [128, 4, 128])
            tmpg = sb("tmpg", [128, 512])
            yacc = [sb(f"yacc{j}", [128, 2048]) for j in range(2)]
            pAc = ps("pAc", [128, 512])
            pCB = ps("pCB", [128, 4, 128])
            pBc = [ps(f"pBc{j}", [128, 4, 128]) for j in range(2)]
            pYI = [ps(f"pYI{j}", [128, 512]) for j in range(2)]
            pYS = ps("pYS", [128, 512])
            pSt = ps("pSt", [128, 512])
            if final:
                Dsk = sb("Dsk", [128, 32]); snw = sb("snw", [128, 16])
                S.dma("sp", Dsk[:], self.dsk, writes=[Dsk])
                S.dma("sp", snw[:], self.snw, writes=[snw])
                wout = sb("wout1", [128, 16, D], BF16)
                self.load_w_bf16(wout, self.w_out1, 16, D)
                Gbc = sb("G1bc", [128, D])
                pYv = None
                yb = sb("yb", [128, 2048]); yg = sb("yg", [128, 2048]); xsD = sb("xsD", [128, 2048])
                yT = sb("yT", [128, 16, 128], BF16)
                xr = sb("xr", [128, D]); tmp = sb("tmp", [128, D]); xo = [sb(f"xo{j}", [128, D]) for j in range(2)]
                sq = sb("sq", [128, 2048], BF16); ss1 = sb("ss1", [128, 1]); ssa = sb("ssa", [128, 1])
                ssb = sb("ssb", [128, 1]); sc = sb("sc", [128, 1])
                tmpb = [sb(f"bct{j}", [128, 128]) for j in range(2)]
                src = self.G1[li][0]
                for k in range(8):
                    t = tmpb[k % 2]
                    pb = pBc[k // 4]
                    self.ts("dve", t[:], self.ones[:], src[:, k:k + 1], None, ALU.mult, None, [self.ones, src], [t])
                    self.mm([(pb[:, k % 4, :], t[:], self.idf[:], True, True)], [t, self.idf], [pb])
                for n in range(2):
                    self.cp("act", Gbc[:, n * 512:(n + 1) * 512].rearrange("p (a b) -> p a b", b=128), pBc[n][:],
                            [pBc[n]], [Gbc])

            for kk, i in enumerate(order):
                S.mark(4000)
                need_y = i >= 2
                rt = rec[kk % 2]; rf = recf[kk % 2]; d_t = dtt[kk % 2]
                S.dma("sp", rt[:], self.rec_tok[i], reads=[self.b_rec[i]], writes=[rt])
                S.dma("sp", rf[:], self.rec_fm[i], reads=[self.b_rec[i]], writes=[rf])
                S.dma("sp", d_t[:], self.dt_d[i], reads=[self.b_rec[i]], writes=[d_t])
                dsl = d_t[:, d * 32:(d + 1) * 32]
                self.tt("dve", a_t[:], dsl, Abc[:, d * 32:(d + 1) * 32], ALU.mult, [d_t, Abc], [a_t])
                self.mm([(pAc[:, 0:32], Md, a_t[:], True, True), (pAc[:, 32:64], self.ones[:], a_t[:], True, True)],
                        [msk, a_t, self.ones], [pAc])
                self.cp("dve", acum[:], pAc[:, 0:32], [pAc], [acum])
                self.cp("dve", tot[:], pAc[:, 32:64], [pAc], [tot])
                self.ts("dve", nacum[:], acum[:], -1.0, None, ALU.mult, None, [acum], [nacum])
                self.tt("dve", xdt[:].rearrange("p (h q) -> p h q", q=64),
                        rt[:, 0:2048].rearrange("p (h q) -> p h q", q=64),
                        dsl.unsqueeze(2).to_broadcast([128, 32, 64]), ALU.mult, [rt, d_t], [xdt])
                ya = yacc[kk % 2]
                if need_y:
                    self.act(E_t[:], acum[:], AF.Exp, [acum], [E_t])
                    self.mm([(pCB[:, g, :], rf[:, g, :], rf[:, 4 + g, :], True, True) for g in range(4)], [rf], [pCB])
                    self.cp("act", cb_sb[:], pCB[:], [pCB], [cb_sb])
                    def stA(t):
                        h0 = t * 4
                        r4 = rhs4[t % 2]; pb = pBc[t % 2]
                        self.tt("dve", r4[:], Md.unsqueeze(1).to_broadcast([128, 4, 128]),
                                a_t[:, h0:h0 + 4].unsqueeze(2).to_broadcast([128, 4, 128]), ALU.mult,
                                [msk, a_t], [r4])
                        self.mm([(pb[:], self.ones[:], r4[:], True, True)], [self.ones, r4], [pb])

                    def stB(t):
                        h0 = t * 4
                        pb = pBc[t % 2]; dd = Dp[t % 2]; lt = Lt[t % 2]
                        self.tt("dve", dd[:], pb[:], Neg.unsqueeze(1).to_broadcast([128, 4, 128]), ALU.add,
                                [pb, msk], [dd])
                        for hh in range(4):
                            self.act(lt[:, hh, :], dd[:, hh, :], AF.Exp, [dd, nacum], [lt],
                                     bias=nacum[:, h0 + hh:h0 + hh + 1])

                    def stC(t):
                        h0 = t * 4
                        g = t // 2
                        lt = Lt[t % 2]; w = Wt[t % 2]
                        self.tt("dve", w[:], lt[:], cb_sb[:, g, :].unsqueeze(1).to_broadcast([128, 4, 128]),
                                ALU.mult, [lt, cb_sb], [w])
                        pyi = pYI[g % 2]
                        self.mm([(pyi[:, ((h0 + hh) % 8) * 64:((h0 + hh) % 8 + 1) * 64], w[:, hh, :],
                                  xdt[:, (h0 + hh) * 64:(h0 + hh + 1) * 64], True, True) for hh in range(4)],
                                [w, xdt], [pyi])
                        if t % 2 == 1:
                            self.mm([(pYS[:], rf[:, 4 + g, :], S_b[:, g * 512:(g + 1) * 512], True, True)],
                                    [rf, S_b], [pYS])
                            self.tt("dve", tmpg[:].rearrange("p (h q) -> p h q", q=64),
                                    pYS[:].rearrange("p (h q) -> p h q", q=64),
                                    E_t[:, g * 8:(g + 1) * 8].unsqueeze(2).to_broadcast([128, 8, 64]), ALU.mult,
                                    [pYS, E_t], [tmpg])
                            self.tt("dve", ya[:, g * 512:(g + 1) * 512], tmpg[:], pyi[:], ALU.add, [tmpg, pyi], [ya])

                    for t in range(10):
                        if t < 8:
                            stA(t)
                        if 1 <= t <= 8:
                            stB(t - 1)
                        if t >= 2:
                            stC(t - 2)
                self.tt("dve", dte[:], tot[:], acum[:], ALU.subtract, [tot, acum], [dte])
                self.act(dte[:], dte[:], AF.Exp, [dte], [dte])
                self.act(etot[:], tot[:], AF.Exp, [tot], [etot])
                self.tt("dve", xdt2[:].rearrange("p (h q) -> p h q", q=64),
                        xdt[:].rearrange("p (h q) -> p h q", q=64),
                        dte[:].unsqueeze(2).to_broadcast([128, 32, 64]), ALU.mult, [xdt, dte], [xdt2])
                for g in range(4):
                    self.mm([(pSt[:], rt[:, 2048 + g * 128:2048 + (g + 1) * 128], xdt2[:, g * 512:(g + 1) * 512],
                              True, True)], [rt, xdt2], [pSt])
                    sg_ = S_f[:, g * 512:(g + 1) * 512]
                    self.tt("dve", sg_.rearrange("p (h q) -> p h q", q=64), sg_.rearrange("p (h q) -> p h q", q=64),
                            etot[:, g * 8:(g + 1) * 8].unsqueeze(2).to_broadcast([128, 8, 64]), ALU.mult,
                            [S_f, etot], [S_f])
                    self.tt("dve", sg_, sg_, pSt[:], ALU.add, [S_f, pSt], [S_f])
                    self.cp("act", S_b[:, g * 512:(g + 1) * 512], sg_, [S_f], [S_b])
                if not need_y:
                    continue
                if not final:
                    S.dma(STQ, self.yb_d[i], ya[:], reads=[ya], writes=[self.b_yb[i]])
                    continue
                S.dma("sp", yb[:], self.yb_d[i], reads=[self.b_yb[i]], writes=[yb])
                S.dma("sp", xr[:], self.xb[i * 128:(i + 1) * 128, :], reads=[self.b_xb[i]], writes=[xr])
                self.tt("dve", yg[:], ya[:], yb[:], ALU.add, [ya, yb], [yg])
                self.tt("dve", xsD[:].rearrange("p (h q) -> p h q", q=64),
                        rt[:, 0:2048].rearrange("p (h q) -> p h q", q=64),
                        Dsk[:].unsqueeze(2).to_broadcast([128, 32, 64]), ALU.mult, [rt, Dsk], [xsD])
                self.tt("dve", yg[:], yg[:], xsD[:], ALU.add, [yg, xsD], [yg])
                self.tt("dve", yg[:], yg[:], rt[:, 2560:4608], ALU.mult, [yg, rt], [yg])
                self.rstd(yg[:], 2048, sq[:], ss1, [yg], [sq])
                for c4 in range(4):
                    pt = pYI[c4 % 2]
                    self.trn([(pt[:, jj * 128:(jj + 1) * 128], yg[:, (c4 * 4 + jj) * 128:(c4 * 4 + jj + 1) * 128],
                               self.idf[:]) for jj in range(4)], [yg, self.idf], [pt])
                    for jj in range(4):
                        c = c4 * 4 + jj
                        self.act(yT[:, c, :], pt[:, jj * 128:(jj + 1) * 128], AF.Identity, [pt, snw], [yT],
                                 scale=snw[:, c:c + 1])
                for n in range(2):
                    self.mm([(pBc[n][:].rearrange("p a b -> p (a b)"), yT[:, k, :], wout[:, k, n * 512:(n + 1) * 512],
                              k == 0, k == 15) for k in range(16)], [yT, wout], [pBc[n]])
                self.act(sq[:, 0:512], pBc[0][:].rearrange("p a b -> p (a b)"), AF.Square, [pBc[0]], [sq, ssa],
                         accum_out=ssa[:, 0:1])
                self.act(sq[:, 512:1024], pBc[1][:].rearrange("p a b -> p (a b)"), AF.Square, [pBc[1]], [sq, ssb],
                         accum_out=ssb[:, 0:1])
                self.tt("dve", ssa[:], ssa[:], ssb[:], ALU.add, [ssa, ssb], [ssa])
                self.stt(ssa[:], ssa[:], 1.0 / D, ss1[:], ALU.mult, ALU.mult, [ssa, ss1], [ssa])
                self.stt(ssa[:], ssa[:], 1.0, ss1[:], ALU.mult, ALU.mult, [ssa, ss1], [ssa])
                self.ts("dve", ssa[:], ssa[:], EPS, None, ALU.add, None, [ssa], [ssa])
                self.act(ssa[:], ssa[:], AF.Sqrt, [ssa], [ssa])
                self.S.op("dve", lambda e: e.reciprocal(out=ssa[:], in_=ssa[:]), [ssa], [ssa])
                self.tt("dve", sc[:], ssa[:], ss1[:], ALU.mult, [ssa, ss1], [sc])
                for n in range(2):
                    self.stt(tmp[:, n * 512:(n + 1) * 512], pBc[n][:].rearrange("p a b -> p (a b)"), sc[:, 0:1],
                             Gbc[:, n * 512:(n + 1) * 512], ALU.mult, ALU.mult, [pBc[n], sc, Gbc], [tmp])
                x_o = xo[kk % 2]
                self.tt("pool", x_o[:], tmp[:], xr[:], ALU.add, [tmp, xr], [x_o])
                S.dma(STQ, self.xm1[i * 128:(i + 1) * 128, :], x_o[:], reads=[x_o], writes=[self.b_xm1[i]])


def prep_inputs(inputs, b):
    f32 = np.float32
    g = lambda k: np.asarray(inputs[k], f32)
    m = {}
    m["xa"] = np.ascontiguousarray(np.concatenate([g("ctx")[b], g("x")[b]], axis=0))
    cond = np.stack([g("c")[b], g("c_ctx")], axis=-1)
    m["cond"] = np.ascontiguousarray(cond.reshape(8, 128, 2).transpose(1, 0, 2))
    m["ada_w"] = g("ada_w")
    m["ada_b"] = np.ascontiguousarray(g("ada_b").reshape(2, 48, 128).transpose(0, 2, 1))
    m["nw"] = np.ascontiguousarray(g("norm_w").reshape(2, 4, 8, 128).transpose(3, 0, 1, 2))
    m["ident"] = np.eye(128, dtype=f32)
    m["w_in0"] = g("attn_w_in")[0]
    m["w_out0"] = g("attn_w_out")[0]
    pw = g("pool_w")[0]
    pwblk = np.zeros((128, 2, 128), f32)
    for gg in range(4):
        c, h = gg // 2, gg % 2
        pwblk[h * 64:(h + 1) * 64, c, h * 64:(h + 1) * 64] = pw[gg]
    m["pwblk"] = pwblk
    m["pscale"] = _fm(g("pool_scale")[0], 2)
    m["qkg"] = np.ascontiguousarray(np.broadcast_to(
        np.stack([g("q_gain")[0], g("k_gain")[0]], 0)[None], (128, 2, 128)))
    m["bands"] = np.ascontiguousarray(_pool_bands().reshape(20, 128, 128).transpose(1, 0, 2))
    cos, sin = _rope_tables()
    m["cos"] = cos
    m["sin"] = sin
    m["w_up"] = g("ffn_w_up")
    m["w_dn"] = g("ffn_w_down")
    m["fcw"] = np.ascontiguousarray(g("ffn_conv_w").reshape(2, 2 * NFF, 128, 3).transpose(0, 2, 1, 3))
    m["fcb"] = np.ascontiguousarray(g("ffn_conv_b").reshape(2, 2 * NFF, 128).transpose(0, 2, 1))
    m["w_in1"] = g("ssm_w_in")[0]
    m["w_out1"] = g("ssm_w_out")[0]
    m["scw"] = np.ascontiguousarray(g("ssm_conv_w")[0].reshape(24, 128, 4).transpose(1, 0, 2))
    m["scb"] = _fm(g("ssm_conv_b")[0], 24)
    bc = lambda v: np.ascontiguousarray(np.broadcast_to(np.asarray(v, f32).reshape(1, -1), (128, v.size)))
    m["dtb"] = bc(g("ssm_dt_bias")[0])
    m["alog"] = bc(g("ssm_A_log")[0])
    m["dsk"] = bc(g("ssm_D")[0])
    m["snw"] = _fm(g("ssm_norm_w")[0], 16)
    ii = np.arange(128)
    mf = (ii[:, None] <= ii[None, :]).astype(f32)
    mb = (ii[:, None] >= ii[None, :]).astype(f32)
    m["msk"] = np.ascontiguousarray(np.stack([mf, mb, (mf - 1) * 30000.0, (mb - 1) * 30000.0], axis=1))
    return m


_NC_CACHE = {}


def kernel(**inputs):
    key = "full"
    if key not in _NC_CACHE:
        _NC_CACHE[key] = MK().build()
    nc = _NC_CACHE[key]
    in_maps = [prep_inputs(inputs, b) for b in range(8)]
    res = run_bass_kernel_spmd(nc, in_maps, core_ids=list(range(8)))
    return np.stack([np.asarray(r["out"], np.float32) for r in res.results], axis=0)
```
